# Optimizing a Trainium2 kernel written in Bass

```python
import math
import jax, jax.numpy as jnp
from jax import lax
import numpy as np

D_MODEL = 1024
BATCH = 8
SEQ = 2048
DEPTH = 4
DEC_BATCH = 128
DEC_SEQ = 4
PAST_LEN = 16384
PAGE_SIZE = 128

D_PLE = 256
D_FF = 2048
S5_WIDTH = 512
S5_GROUP = 16
S5_GROUPS = S5_WIDTH // S5_GROUP
S5_STATE = 64
MLSTM_WIDTH = 1024
MLSTM_HEADS = 4
MLSTM_HEAD_DIM = MLSTM_WIDTH // MLSTM_HEADS
MLSTM_CHUNK = 128
CONV_WIDTH = 4
N_BRANCHES = 2
IN_COLS = S5_WIDTH + 4 * MLSTM_WIDTH + 2 * MLSTM_HEADS + N_BRANCHES * D_MODEL
EPS = 1e-6

kernel_name = 'hybrid_s5_mlstm_macaron_step'


def _rmsnorm(x, g):
    xf = x.astype(jnp.float32)
    y = xf * lax.rsqrt(jnp.mean(xf * xf, axis=-1, keepdims=True) + EPS) * g.astype(jnp.float32)
    return y.astype(x.dtype)


def _swiglu(x, w_gate, w_up, w_down):
    return (jax.nn.silu(x @ w_gate) * (x @ w_up)) @ w_down


def _combine(a, b):
    ar, ai, br, bi = a
    cr, ci, dr, di = b
    return (cr * ar - ci * ai, cr * ai + ci * ar, cr * br - ci * bi + dr, cr * bi + ci * br + di)


def _s5(u, h0_re, h0_im, lam_re, lam_im, log_dt, b_re, b_im, c_re, c_im, d_skip):
    f32 = jnp.float32
    Bsz, S, _ = u.shape
    uf = u.astype(f32).reshape(Bsz, S, S5_GROUPS, S5_GROUP)
    lre = lam_re.astype(f32)
    lim = lam_im.astype(f32)
    dt = jnp.exp(log_dt.astype(f32))[:, None]
    mag = jnp.exp(lre * dt)
    abar_re = mag * jnp.cos(lim * dt)
    abar_im = mag * jnp.sin(lim * dt)
    den = lre * lre + lim * lim
    pr = abar_re - 1.0
    w_re = (pr * lre + abar_im * lim) / den
    w_im = (abar_im * lre - pr * lim) / den
    br_ = b_re.astype(f32)
    bi_ = b_im.astype(f32)
    bb_re = w_re[..., None] * br_ - w_im[..., None] * bi_
    bb_im = w_re[..., None] * bi_ + w_im[..., None] * br_
    bu_re = jnp.einsum('bsgh,gnh->bsgn', uf, bb_re)
    bu_im = jnp.einsum('bsgh,gnh->bsgn', uf, bb_im)
    h0r = h0_re.astype(f32)
    h0i = h0_im.astype(f32)
    bu_re = bu_re.at[:, 0].add(abar_re * h0r - abar_im * h0i)
    bu_im = bu_im.at[:, 0].add(abar_re * h0i + abar_im * h0r)
    ar = jnp.broadcast_to(abar_re, bu_re.shape)
    ai = jnp.broadcast_to(abar_im, bu_re.shape)
    _, _, hr, hi = lax.associative_scan(_combine, (ar, ai, bu_re, bu_im), axis=1)
    y = (jnp.einsum('bsgn,ghn->bsgh', hr, c_re.astype(f32))
         - jnp.einsum('bsgn,ghn->bsgh', hi, c_im.astype(f32))
         + d_skip.astype(f32).reshape(S5_GROUPS, S5_GROUP) * uf)
    return y.reshape(Bsz, S, S5_WIDTH), hr[:, -1], hi[:, -1]


def _causal_conv(x_raw, buf, w, b):
    S = x_raw.shape[1]
    xp = jnp.concatenate([buf.astype(x_raw.dtype), x_raw], axis=1)
    out = b.astype(x_raw.dtype)
    for j in range(CONV_WIDTH):
        out = out + xp[:, j:j + S] * w[j].astype(x_raw.dtype)
    return out, xp[:, S:]


def _mlstm(q, k, v, log_i, log_f, c0, n0, m0):
    Bsz, S, H, DH = q.shape
    L = math.gcd(S, MLSTM_CHUNK)
    NC = S // L

    def chunks(t):
        t = t.reshape((Bsz, NC, L) + t.shape[2:])
        return jnp.moveaxis(jnp.moveaxis(t, 1, 0), 3, 2)

    tril = jnp.tril(jnp.ones((L, L), dtype=bool))

    def step(carry, inp):
        C, n, m = carry
        qc, kc, vc, li, lf = inp
        b = jnp.cumsum(lf, axis=-1)
        dmat = b[..., :, None] - b[..., None, :] + li[..., None, :]
        dmat = jnp.where(tril, dmat, -jnp.inf)
        inter = b + m[..., None]
        m_t = jnp.maximum(inter, jnp.max(dmat, axis=-1))
        w = jnp.exp(dmat - m_t[..., None])
        scores = jnp.einsum('bhtd,bhsd->bhts', qc, kc) * w
        a = jnp.exp(inter - m_t)
        num = (jnp.einsum('bhts,bhsd->bhtd', scores, vc)
               + a[..., None] * jnp.einsum('bhtk,bhkv->bhtv', qc, C))
        nq = jnp.sum(scores, axis=-1) + a * jnp.einsum('bhtk,bhk->bht', qc, n)
        h = num / jnp.maximum(jnp.abs(nq), jnp.exp(-m_t))[..., None]
        b_last = b[..., -1]
        g = b_last[..., None] - b + li
        m_new = jnp.maximum(b_last + m, jnp.max(g, axis=-1))
        wk = jnp.exp(g - m_new[..., None])
        decay = jnp.exp(b_last + m - m_new)
        C_new = decay[..., None, None] * C + jnp.einsum('bhs,bhsk,bhsv->bhkv', wk, kc, vc)
        n_new = decay[..., None] * n + jnp.einsum('bhs,bhsk->bhk', wk, kc)
        return (C_new, n_new, m_new), h

    carry0 = (c0.astype(jnp.float32), n0.astype(jnp.float32), m0.astype(jnp.float32))
    (C, n, m), hs = lax.scan(step, carry0, (chunks(q), chunks(k), chunks(v), chunks(log_i), chunks(log_f)))
    h = jnp.transpose(hs, (1, 0, 3, 2, 4)).reshape(Bsz, S, H, DH)
    return h, C, n, m


def _mixer(h, s5_re0, s5_im0, c0, n0, m0, conv0, lw):
    f32 = jnp.float32
    Bsz, S, _ = h.shape
    sizes = (S5_WIDTH, 2 * MLSTM_WIDTH, MLSTM_WIDTH, MLSTM_WIDTH, MLSTM_HEADS, MLSTM_HEADS, D_MODEL, D_MODEL)
    cuts = [int(c) for c in np.cumsum(sizes)[:-1]]
    u, qk_raw, v, o, ig, fg, gate_s5, gate_m = jnp.split(h @ lw['w_in'], cuts, axis=-1)
    y_s5, s5_re, s5_im = _s5(u, s5_re0, s5_im0, lw['s5_lambda_re'], lw['s5_lambda_im'], lw['s5_log_dt'],
                             lw['s5_b_re'], lw['s5_b_im'], lw['s5_c_re'], lw['s5_c_im'], lw['s5_d'])
    y_s5 = jax.nn.gelu(y_s5.astype(h.dtype))
    y_s5 = y_s5 * jax.nn.sigmoid(y_s5 @ lw['s5_w_glu'])
    qk, conv_new = _causal_conv(qk_raw, conv0, lw['conv_w'], lw['conv_b'])
    q, k = jnp.split(jax.nn.silu(qk), 2, axis=-1)
    shp = (Bsz, S, MLSTM_HEADS, MLSTM_HEAD_DIM)
    q = q.reshape(shp).astype(f32) * (MLSTM_HEAD_DIM ** -0.5)
    k = k.reshape(shp).astype(f32)
    v = v.reshape(shp).astype(f32)
    log_i = ig.astype(f32) + lw['b_igate'].astype(f32)
    log_f = jax.nn.log_sigmoid(fg.astype(f32) + lw['b_fgate'].astype(f32))
    hm, c_new, n_new, m_new = _mlstm(q, k, v, log_i, log_f, c0, n0, m0)
    hm = hm * lax.rsqrt(jnp.mean(hm * hm, axis=-1, keepdims=True) + EPS)
    hm = (hm.reshape(Bsz, S, MLSTM_WIDTH) * lw['g_mhead'].astype(f32)).astype(h.dtype) * jax.nn.sigmoid(o)
    merged = (jax.nn.sigmoid(gate_s5) * (y_s5 @ lw['w_s5_up'])
              + jax.nn.sigmoid(gate_m) * (hm @ lw['w_m_up']))
    return merged @ lw['w_out'], (s5_re, s5_im, c_new, n_new, m_new, conv_new)


def _layer(x, p, s5_re0, s5_im0, c0, n0, m0, conv0, lw):
    x = x + 0.5 * _swiglu(_rmsnorm(x, lw['g_ffn1']), lw['w1_gate'], lw['w1_up'], lw['w1_down'])
    mix, new_state = _mixer(_rmsnorm(x, lw['g_mix']), s5_re0, s5_im0, c0, n0, m0, conv0, lw)
    x = x + mix
    x = x + 0.5 * _swiglu(_rmsnorm(x, lw['g_ffn2']), lw['w2_gate'], lw['w2_up'], lw['w2_down'])
    gate = jax.nn.sigmoid(_rmsnorm(x, lw['g_ple']) @ lw['w_ple_gate'])
    x = x + (p.astype(x.dtype) @ lw['w_ple']) * gate
    return x, new_state


def setup_inputs(seed: int = 0) -> dict:
    key = jax.random.key(seed)
    ks = jax.random.split(key, 48)
    keys = [ks[i] for i in range(48)]

    def nk():
        return keys.pop(0)

    def nrm(shape, scale=1.0):
        return jax.random.normal(nk(), shape, jnp.float32) * scale

    def gain(shape):
        return 1.0 + nrm(shape, 0.02)

    L = DEPTH
    G, N = S5_GROUPS, S5_STATE
    H, DH = MLSTM_HEADS, MLSTM_HEAD_DIM
    n_idx = jnp.arange(N, dtype=jnp.float32)
    return {
        'x_prompt': nrm((BATCH, SEQ, D_MODEL)),
        'x_sample': nrm((DEC_BATCH, DEC_SEQ, D_MODEL)),
        'state_s5_re': nrm((L, DEC_BATCH, G, N), 0.1),
        'state_s5_im': nrm((L, DEC_BATCH, G, N), 0.1),
        'state_mlstm_c': nrm((L, DEC_BATCH, H, DH, DH), 0.1),
        'state_mlstm_n': nrm((L, DEC_BATCH, H, DH), 0.5),
        'state_mlstm_m': nrm((L, DEC_BATCH, H), 1.0),
        'state_conv': nrm((L, DEC_BATCH, CONV_WIDTH - 1, 2 * MLSTM_WIDTH)),
        'p_prompt': nrm((L, BATCH, SEQ, D_PLE)),
        'p_sample': nrm((L, DEC_BATCH, DEC_SEQ, D_PLE)),
        'g_ffn1': gain((L, D_MODEL)),
        'w1_gate': nrm((L, D_MODEL, D_FF), D_MODEL ** -0.5),
        'w1_up': nrm((L, D_MODEL, D_FF), D_MODEL ** -0.5),
        'w1_down': nrm((L, D_FF, D_MODEL), D_FF ** -0.5),
        'g_mix': gain((L, D_MODEL)),
        'w_in': nrm((L, D_MODEL, IN_COLS), D_MODEL ** -0.5),
        's5_lambda_re': -0.5 + nrm((L, G, N), 0.01),
        's5_lambda_im': math.pi * n_idx + nrm((L, G, N), 0.01),
        's5_log_dt': jax.random.uniform(nk(), (L, G), jnp.float32, math.log(1e-3), math.log(1e-1)),
        's5_b_re': nrm((L, G, N, S5_GROUP), S5_GROUP ** -0.5),
        's5_b_im': nrm((L, G, N, S5_GROUP), S5_GROUP ** -0.5),
        's5_c_re': nrm((L, G, S5_GROUP, N), N ** -0.5),
        's5_c_im': nrm((L, G, S5_GROUP, N), N ** -0.5),
        's5_d': nrm((L, S5_WIDTH)),
        's5_w_glu': nrm((L, S5_WIDTH, S5_WIDTH), S5_WIDTH ** -0.5),
        'w_s5_up': nrm((L, S5_WIDTH, D_MODEL), S5_WIDTH ** -0.5),
        'conv_w': nrm((L, CONV_WIDTH, 2 * MLSTM_WIDTH), CONV_WIDTH ** -0.5),
        'conv_b': nrm((L, 2 * MLSTM_WIDTH), 0.02),
        'b_igate': nrm((L, H), 0.1),
        'b_fgate': jax.random.uniform(nk(), (L, H), jnp.float32, 3.0, 6.0),
        'g_mhead': gain((L, MLSTM_WIDTH)),
        'w_m_up': nrm((L, MLSTM_WIDTH, D_MODEL), MLSTM_WIDTH ** -0.5),
        'w_out': nrm((L, D_MODEL, D_MODEL), D_MODEL ** -0.5),
        'g_ffn2': gain((L, D_MODEL)),
        'w2_gate': nrm((L, D_MODEL, D_FF), D_MODEL ** -0.5),
        'w2_up': nrm((L, D_MODEL, D_FF), D_MODEL ** -0.5),
        'w2_down': nrm((L, D_FF, D_MODEL), D_FF ** -0.5),
        'g_ple': gain((L, D_MODEL)),
        'w_ple': nrm((L, D_PLE, D_MODEL), D_PLE ** -0.5),
        'w_ple_gate': nrm((L, D_MODEL, D_MODEL), D_MODEL ** -0.5),
        'g_final': gain((D_MODEL,)),
    }


def reference(x_prompt, x_sample, state_s5_re, state_s5_im, state_mlstm_c, state_mlstm_n, state_mlstm_m,
              state_conv, p_prompt, p_sample, g_ffn1, w1_gate, w1_up, w1_down, g_mix, w_in,
              s5_lambda_re, s5_lambda_im, s5_log_dt, s5_b_re, s5_b_im, s5_c_re, s5_c_im, s5_d, s5_w_glu,
              w_s5_up, conv_w, conv_b, b_igate, b_fgate, g_mhead, w_m_up, w_out, g_ffn2, w2_gate, w2_up,
              w2_down, g_ple, w_ple, w_ple_gate, g_final):
    f32 = jnp.float32
    Bp = x_prompt.shape[0]
    H, DH = MLSTM_HEADS, MLSTM_HEAD_DIM
    z_s5 = jnp.zeros((Bp, S5_GROUPS, S5_STATE), f32)
    z_c = jnp.zeros((Bp, H, DH, DH), f32)
    z_n = jnp.zeros((Bp, H, DH), f32)
    z_m = jnp.zeros((Bp, H), f32)
    z_conv = jnp.zeros((Bp, CONV_WIDTH - 1, 2 * MLSTM_WIDTH), x_prompt.dtype)
    outs_p = [[] for _ in range(6)]
    outs_s = [[] for _ in range(6)]
    xp, xs = x_prompt, x_sample
    for i in range(DEPTH):
        lw = {
            'g_ffn1': g_ffn1[i], 'w1_gate': w1_gate[i], 'w1_up': w1_up[i], 'w1_down': w1_down[i],
            'g_mix': g_mix[i], 'w_in': w_in[i],
            's5_lambda_re': s5_lambda_re[i], 's5_lambda_im': s5_lambda_im[i], 's5_log_dt': s5_log_dt[i],
            's5_b_re': s5_b_re[i], 's5_b_im': s5_b_im[i], 's5_c_re': s5_c_re[i], 's5_c_im': s5_c_im[i],
            's5_d': s5_d[i], 's5_w_glu': s5_w_glu[i], 'w_s5_up': w_s5_up[i],
            'conv_w': conv_w[i], 'conv_b': conv_b[i], 'b_igate': b_igate[i], 'b_fgate': b_fgate[i],
            'g_mhead': g_mhead[i], 'w_m_up': w_m_up[i], 'w_out': w_out[i],
            'g_ffn2': g_ffn2[i], 'w2_gate': w2_gate[i], 'w2_up': w2_up[i], 'w2_down': w2_down[i],
            'g_ple': g_ple[i], 'w_ple': w_ple[i], 'w_ple_gate': w_ple_gate[i],
        }
        xp, st_p = _layer(xp, p_prompt[i], z_s5, z_s5, z_c, z_n, z_m, z_conv, lw)
        xs, st_s = _layer(xs, p_sample[i], state_s5_re[i], state_s5_im[i], state_mlstm_c[i],
                          state_mlstm_n[i], state_mlstm_m[i], state_conv[i], lw)
        for j in range(6):
            outs_p[j].append(st_p[j])
            outs_s[j].append(st_s[j])
    y_prompt = _rmsnorm(xp, g_final)
    y_sample = _rmsnorm(xs, g_final)
    return (y_prompt, y_sample,
            jnp.stack(outs_p[0]), jnp.stack(outs_p[1]), jnp.stack(outs_p[2]),
            jnp.stack(outs_p[3]), jnp.stack(outs_p[4]), jnp.stack(outs_p[5]),
            jnp.stack(outs_s[0]), jnp.stack(outs_s[1]), jnp.stack(outs_s[2]),
            jnp.stack(outs_s[3]), jnp.stack(outs_s[4]), jnp.stack(outs_s[5]))
```

```python
import math
import numpy as np
from contextlib import ExitStack
import concourse.bass as bass
import concourse.mybir as mybir
from concourse.bass_utils import run_bass_kernel_spmd

F32 = mybir.dt.float32
BF16 = mybir.dt.bfloat16
I32 = mybir.dt.int32
AF = mybir.ActivationFunctionType
ALU = mybir.AluOpType

D = 1024; DFF = 2048; DPLE = 256; NL = 4; G = 32; NS5 = 64; H = 4; DH = 256
IN_COLS = 6664
C_U = 0; C_QK = 512; C_V = 2560; C_O = 3584; C_IG = 4608; C_FG = 4612; C_GS5 = 4616; C_GM = 5640
EPS = 1e-6
LN16 = math.log(16.0)
TWO_PI = 2.0 * math.pi
NCORES = 8
SEQ = 2048
TPP = 1024
NSAMP = 64
_ISZ = {F32: 4, BF16: 2, I32: 4}
MC_D = 0; MC_CW = 4; MC_CB = 68; MC_GMH = 84; MC_BI = 92; MC_BF = 93; MIXC_N = 96
CC_ID = 0; CC_J = 128; CC_OH = 256; CC_BM = 768; CC_BMS = 896; CC_BLK = 960; CC_M01 = 976; CC_M4 = 2064; CC_SEL = 2128; CC_N = 2192


class _Op:
    __slots__ = ("eng", "fn", "deps", "is_dma", "sem_key", "sem_val", "has_dep", "idx", "dma_sem", "prev_total")

    def __init__(self, eng, fn, is_dma, sem_key):
        self.eng = eng; self.fn = fn; self.deps = set(); self.is_dma = is_dma
        self.sem_key = sem_key; self.sem_val = None; self.has_dep = False; self.dma_sem = None; self.prev_total = 0


def _regions(ap):
    isz = _ISZ[ap.dtype]
    dims = ap.ap
    ps, pc = dims[0]
    off = ap.offset
    if ps > 0:
        p_lo = off // ps
        col = off - p_lo * ps
    else:
        p_lo = 0; col = off
    free = [(s, n) for (s, n) in dims[1:] if n > 1 and s != 0]
    if not free:
        return ap.tensor.name, p_lo, p_lo + pc, [(col * isz, (col + 1) * isz)]
    free.sort(key=lambda d: abs(d[0]))
    s0, n0 = free[0]
    if s0 == 1:
        run = n0; outer = free[1:]
    else:
        run = 1; outer = free
    nout = 1
    for s, n in outer:
        nout *= n
    if nout > 48:
        lo = col; hi = col + run
        for s, n in outer:
            hi += s * (n - 1)
        return ap.tensor.name, p_lo, p_lo + pc, [(lo * isz, hi * isz)]
    starts = [col]
    for s, n in outer:
        starts = [b + s * i for b in starts for i in range(n)]
    starts.sort()
    rngs = []
    for b in starts:
        lo = b * isz; hi = (b + run) * isz
        if rngs and rngs[-1][1] >= lo:
            rngs[-1] = (rngs[-1][0], max(hi, rngs[-1][1]))
        else:
            rngs.append((lo, hi))
    return ap.tensor.name, p_lo, p_lo + pc, rngs


class Sched:
    ENGS = ("pe", "act", "dve", "pool", "sp")
    HAND = {"pe": "tensor", "act": "scalar", "dve": "vector", "pool": "gpsimd", "sp": "sync"}

    def __init__(self, nc, n_dma_sems=80):
        self.nc = nc
        self.ops = []
        self.recs = {}
        self.keyw = {}; self.keyr = {}
        self.n_dma_sems = n_dma_sems
        self.dry = False
        self._rc = {}

    def _reg(self, ap):
        k = (ap.tensor.name, ap.offset, ap.ap, ap.dtype)
        r = self._rc.get(k)
        if r is None:
            r = _regions(ap)
            self._rc[k] = r
        return r

    def op(self, eng, fn, reads=(), writes=(), dma=False, sem_key=None):
        if self.dry:
            return None
        o = _Op(eng, fn, dma, sem_key)
        o.idx = len(self.ops)
        deps = o.deps
        rr = []; ww = []
        for a in reads:
            if isinstance(a, (str, tuple)):
                w = self.keyw.get(a)
                if w is not None: deps.add(w)
                self.keyr.setdefault(a, []).append(o.idx)
            else:
                rr.append(self._reg(a))
        for a in writes:
            if isinstance(a, (str, tuple)):
                w = self.keyw.get(a)
                if w is not None: deps.add(w)
                for r in self.keyr.get(a, ()): deps.add(r)
                self.keyw[a] = o.idx; self.keyr[a] = []
            else:
                ww.append(self._reg(a))
        for (name, p0, p1, rngs) in rr:
            lst = self.recs.setdefault(name, [])
            for (lo, hi) in rngs:
                exact = None
                for rec in lst:
                    if rec[2] < hi and lo < rec[3] and rec[0] < p1 and p0 < rec[1]:
                        if rec[4] is not None: deps.add(rec[4])
                        if rec[0] == p0 and rec[1] == p1 and rec[2] == lo and rec[3] == hi:
                            exact = rec
                if exact is not None:
                    exact[5].append(o.idx)
                else:
                    lst.append([p0, p1, lo, hi, None, [o.idx]])
        for (name, p0, p1, rngs) in ww:
            lst = self.recs.setdefault(name, [])
            for (lo, hi) in rngs:
                keep = []
                for rec in lst:
                    if rec[2] < hi and lo < rec[3] and rec[0] < p1 and p0 < rec[1]:
                        if rec[4] is not None: deps.add(rec[4])
                        deps.update(rec[5])
                        if rec[0] >= p0 and rec[1] <= p1 and rec[2] >= lo and rec[3] <= hi:
                            continue
                    keep.append(rec)
                keep.append([p0, p1, lo, hi, o.idx, []])
                lst[:] = keep
        deps.discard(o.idx)
        self.ops.append(o)
        return o

    def emit(self, es):
        nc = self.nc
        ops = self.ops
        for o in ops:
            nd = set()
            for d in o.deps:
                do = ops[d]
                if (not do.is_dma) and (not o.is_dma) and do.eng == "pe" and o.eng == "pe":
                    continue
                nd.add(d)
            o.deps = nd
            for d in nd:
                ops[d].has_dep = True
        dma_keys = {}; dma_cnt = {}
        eng_sem = {e: es.enter_context(nc.semaphore("s_" + e)) for e in self.ENGS}
        dma_sems = []
        eng_cnt = {e: 0 for e in self.ENGS}
        for o in ops:
            if o.is_dma:
                k = o.sem_key
                if k not in dma_keys:
                    if len(dma_sems) < self.n_dma_sems:
                        dma_sems.append(es.enter_context(nc.semaphore("d%d" % len(dma_sems))))
                        dma_keys[k] = len(dma_sems) - 1
                    else:
                        dma_keys[k] = len(dma_keys) % self.n_dma_sems
                si = dma_keys[k]
                o.dma_sem = si
                dma_cnt[si] = dma_cnt.get(si, 0) + 16
                o.sem_val = dma_cnt[si]
                o.prev_total = o.sem_val - 16
                o.has_dep = True
            elif o.has_dep:
                eng_cnt[o.eng] += 1
                o.sem_val = eng_cnt[o.eng]
        per_eng = {e: [o for o in ops if o.eng == e] for e in self.ENGS}
        stats = {"waits": 0}

        def run_engine(e, eng):
            known = {}

            def wait(sid, sem, val):
                if val <= 0 or known.get(sid, 0) >= val:
                    return
                eng.wait_ge(sem, val)
                stats["waits"] += 1
                known[sid] = val

            for o in per_eng[e]:
                for d in sorted(o.deps):
                    do = ops[d]
                    if do.is_dma:
                        wait(("d", do.dma_sem), dma_sems[do.dma_sem], do.sem_val)
                    else:
                        wait(("e", do.eng), eng_sem[do.eng], do.sem_val)
                if o.is_dma:
                    wait(("d", o.dma_sem), dma_sems[o.dma_sem], o.prev_total)
                inst = o.fn(eng)
                if o.is_dma:
                    inst.then_inc(dma_sems[o.dma_sem], 16)
                elif o.has_dep:
                    inst.then_inc(eng_sem[e], 1)
            if e == "sp":
                for si, tot in dma_cnt.items():
                    wait(("d", si), dma_sems[si], tot)

        with nc.Block() as block:
            for e in self.ENGS:
                getattr(block, self.HAND[e])(lambda eng, e=e: run_engine(e, eng))
        self.stats = stats


class Prog:
    def __init__(self, cfg=None):
        cfg = dict(cfg or {})
        self.cfg = cfg
        self.nl = cfg.get("n_layers", NL)
        self.passes = cfg.get("passes", ("A", "B"))
        self.stop = cfg.get("stop", None)
        self.use_s5 = cfg.get("use_s5", True)
        self.use_ml = cfg.get("use_ml", True)
        self.nc = bass.Bass("TRN2", target_bir_lowering=False)
        self.es = ExitStack()
        self.S = Sched(self.nc)
        self.dram = {}
        self.ins = []
        self.outs = []

    def din(self, name, shape, dt=F32):
        t = self.nc.dram_tensor(name, list(shape), dt, kind="ExternalInput").ap()
        self.dram[name] = t; self.ins.append(name)
        return t

    def dout(self, name, shape, dt=F32):
        t = self.nc.dram_tensor(name, list(shape), dt, kind="ExternalOutput").ap()
        self.dram[name] = t; self.outs.append(name)
        return t

    def dump(self, name, ap, dt=F32):
        if not self.cfg.get("dump") or self.S.dry or name in self.dram:
            return
        d = self.dout("dbg_" + name, list(ap.shape), dt)
        self.dma("sp", d, ap, ("dbg", name), reads=[ap])

    def sb(self, name, shape, dt=F32):
        return self.es.enter_context(self.nc.sbuf_tensor(name, list(shape), dt))

    def arena_init(self, nbytes):
        self.arena = self.sb("arena", [128, nbytes // 2], BF16)
        self.arena_bytes = nbytes
        self.atop = 0
        self.amax = 0

    def alloc(self, shape_free, dt):
        n = 1
        for s in shape_free: n *= s
        nb = n * _ISZ[dt]
        nb = (nb + 63) // 64 * 64
        off = self.atop
        self.atop += nb
        self.amax = max(self.amax, self.atop)
        assert self.atop <= self.arena_bytes, ("arena overflow", self.atop, self.arena_bytes)
        return self.view(off, shape_free, dt)

    def view(self, off, shape_free, dt):
        n = 1
        for s in shape_free: n *= s
        v = self.arena[:, off // 2: off // 2 + n * _ISZ[dt] // 2]
        if dt != BF16:
            v = v.bitcast(dt)
        if len(shape_free) == 2:
            v = v.rearrange("p (a b) -> p a b", a=shape_free[0])
        elif len(shape_free) == 3:
            v = v.rearrange("p (a b c) -> p a b c", a=shape_free[0], b=shape_free[1])
        return v

    def mark(self):
        return self.atop

    def release(self, m):
        self.atop = m

    def psum_init(self):
        self.banks = [self.es.enter_context(self.nc.psum_tensor("bank%d" % i, [128, 512], F32)) for i in range(8)]
        self.bi = 0
        self.scr = self.banks[7][0:1, 0:1]

    def bank(self):
        b = self.banks[self.bi % 7]
        self.bi += 1
        return b

    def op(self, eng, fn, reads=(), writes=(), **kw):
        return self.S.op(eng, fn, reads=reads, writes=writes, **kw)

    def act(self, out, in_, func, bias=None, scale=None, accum=None, extra_reads=()):
        kw = {}
        rd = [in_] + list(extra_reads)
        if bias is not None:
            kw["bias"] = bias
            if not isinstance(bias, float): rd.append(bias)
        if scale is not None:
            kw["scale"] = scale
            if not isinstance(scale, float): rd.append(scale)
        wr = [out]
        if accum is not None:
            kw["accum_out"] = accum; wr.append(accum)
        self.op("act", lambda e: e.activation(out=out, in_=in_, func=func, **kw), reads=rd, writes=wr)

    def tt(self, eng, out, a, b, op):
        self.op(eng, lambda e: e.tensor_tensor(out=out, in0=a, in1=b, op=op), reads=[a, b], writes=[out])

    def ts(self, eng, out, a, s1, op0, s2=None, op1=None):
        rd = [a]
        if not isinstance(s1, (float, int)): rd.append(s1)
        if s2 is not None and not isinstance(s2, (float, int)): rd.append(s2)
        if op1 is None:
            self.op(eng, lambda e: e.tensor_scalar(out=out, in0=a, scalar1=s1, scalar2=None, op0=op0), reads=rd, writes=[out])
        else:
            self.op(eng, lambda e: e.tensor_scalar(out=out, in0=a, scalar1=s1, scalar2=s2, op0=op0, op1=op1), reads=rd, writes=[out])

    def stt(self, out, a, s, b, op0, op1):
        rd = [a, b]
        if not isinstance(s, (float, int)): rd.append(s)
        self.op("dve", lambda e: e.scalar_tensor_tensor(out=out, in0=a, scalar=s, in1=b, op0=op0, op1=op1), reads=rd, writes=[out])

    def copy(self, eng, out, in_):
        if eng == "act":
            self.act(out, in_, AF.Copy)
        else:
            self.op(eng, lambda e: e.tensor_copy(out=out, in_=in_), reads=[in_], writes=[out])

    def memset(self, eng, out, val):
        self.op(eng, lambda e: e.memset(out, val), writes=[out])

    def recip(self, out, in_):
        self.op("dve", lambda e: e.reciprocal(out=out, in_=in_), reads=[in_], writes=[out])

    def scan(self, out, d0, d1, init, op0, op1):
        rd = [d0, d1]
        if not isinstance(init, (float, int)): rd.append(init)
        self.op("dve", lambda e: e.tensor_tensor_scan(out=out, data0=d0, data1=d1, initial=init, op0=op0, op1=op1), reads=rd, writes=[out])

    def mm(self, out, pairs, extra_reads=()):
        rd = []
        for l, r in pairs:
            rd.append(l); rd.append(r)
        n = len(pairs)
        is32 = pairs[0][0].dtype == F32
        scr = self.scr; idb = self.identb

        def fn(e):
            inst = None
            for i, (l, r) in enumerate(pairs):
                inst = e.matmul(out, lhsT=l, rhs=r, start=(i == 0), stop=(i == n - 1))
            if is32:
                inst = e.matmul(scr, lhsT=idb[:, 0:1], rhs=idb[:, 0:1], start=True, stop=True)
            return inst
        self.op("pe", fn, reads=rd + list(extra_reads), writes=[out])

    def transpose(self, out, in_, ident):
        self.op("pe", lambda e: e.transpose(out, in_, ident), reads=[in_, ident], writes=[out])

    def transposes(self, triples):
        def fn(e):
            inst = None
            for (o, i, d) in triples:
                inst = e.transpose(o, i, d)
            return inst
        self.op("pe", fn, reads=[t[1] for t in triples] + [t[2] for t in triples], writes=[t[0] for t in triples])

    def dma(self, eng, out, in_, key, reads=(), writes=()):
        self.op(eng, lambda e: e.dma_start(out=out, in_=in_), reads=reads, writes=writes, dma=True, sem_key=key)

    def ws_init(self, nslots):
        self.ws_slots = [self.alloc([2048], BF16) for _ in range(nslots)]
        self.ws_n = nslots
        self.ws_plan = []
        self.ws_i = 0
        self.ws_issued = 0
        self.ws_live = 1

    def ws_reset(self):
        self.ws_i = 0
        self.ws_issued = 0

    def _ws_view(self, k, KC, ncols):
        s = self.ws_slots[k % self.ws_n]
        return s[:, 0:KC * ncols].rearrange("p (kc n) -> p kc n", kc=KC)

    def _ws_issue(self, k):
        W, c0, ncols, KC = self.ws_plan[k]
        dst = self._ws_view(k, KC, ncols)
        src = W.rearrange("(kc p) n -> p kc n", p=128)[:, :, c0:c0 + ncols]
        self.dma("pool", dst, src, ("w", k % self.ws_n), writes=[dst])

    def wget(self, W, c0, ncols, KC, oldest=None):
        if self.S.dry:
            self.ws_plan.append((W, c0, ncols, KC))
            k = len(self.ws_plan) - 1
            self.ws_last = k
            return self._ws_view(k, KC, ncols)
        k = self.ws_i
        self.ws_last = k
        assert self.ws_plan[k][1:] == (c0, ncols, KC), (self.ws_plan[k][1:], (c0, ncols, KC))
        old = k if oldest is None else min(oldest, k)
        assert k - old < self.ws_n
        lim = min(len(self.ws_plan), old + self.ws_n)
        while self.ws_issued < lim:
            self._ws_issue(self.ws_issued)
            self.ws_issued += 1
        self.ws_i += 1
        return self._ws_view(k, KC, ncols)

    def dense_multi(self, streams, ncols, consume, tiles=None):
        tiles = tiles or self.tiles
        nj = ncols // 128
        self.ws_live = len(streams)
        blk = [min(2048 // KC, 512) for (_, _, KC, _) in streams]
        blk = [min(blk)] * len(streams)
        cur = [None] * len(streams)
        curk = [None] * len(streams)
        for j in range(nj):
            for si, (W, c0, KC, rhs_fn) in enumerate(streams):
                nw = blk[si]
                if (j * 128) % nw == 0:
                    others = [curk[s2] for s2 in range(len(streams)) if s2 != si and curk[s2] is not None]
                    cur[si] = self.wget(W, c0 + j * 128, min(nw, ncols - j * 128), KC, oldest=min(others) if others else None)
                    curk[si] = self.ws_last
            for ti, (t0, tw) in enumerate(tiles):
                pss = []
                for si, (W, c0, KC, rhs_fn) in enumerate(streams):
                    nw = blk[si]
                    jo = (j * 128) % nw
                    ps = self.bank()[:, 0:tw]
                    self.mm(ps, [(cur[si][:, kc, jo:jo + 128], rhs_fn(kc, t0, tw)) for kc in range(KC)])
                    pss.append(ps)
                consume(j, ti, t0, tw, pss)

    def build(self):
        nc = self.nc
        with self.es:
            self._declare()
            self.S.dry = True
            self._body()
            self.S.dry = False
            self.ws_reset()
            self.atop = self.a_persist
            self.bi = 0
            self._body()
            self.S.emit(self.es)
        return nc

    def _declare(self):
        L = NL
        din = self.din
        self.x_p = din("x_p", [SEQ, D]); self.x_s = din("x_s", [NSAMP, D])
        self.p_p = din("p_p", [L, SEQ, DPLE]); self.p_s = din("p_s", [L, NSAMP, DPLE])
        for n, k, m in [("w1_gate", D, DFF), ("w1_up", D, DFF), ("w1_down", DFF, D), ("w_in", D, IN_COLS),
                        ("s5_w_glu", 512, 512), ("w_s5_up", 512, D), ("w_m_up", D, D), ("w_out", D, D),
                        ("w2_gate", D, DFF), ("w2_up", D, DFF), ("w2_down", DFF, D), ("w_ple", DPLE, D),
                        ("w_ple_gate", D, D)]:
            setattr(self, n, din(n, [L, k, m]))
        self.gcols_d = din("gcols", [128, 4 * L + 1, 8])
        self.consts_d = din("consts", [128, CC_N])
        self.y_p = self.dout("y_p", [SEQ, D]); self.y_s = self.dout("y_s", [NSAMP, D])
        self.s5p_d = din("s5p", [L, 128, 3, 16])
        self.s5B_d = din("s5B", [L, 2, 128, 16, 128]); self.s5C_d = din("s5C", [L, 2, 128, 16, 128])
        self.s5h0_d = din("s5h0", [L, 2, 128, 16, 16])
        self.mixc_d = din("mixc", [128, L, MIXC_N])
        self.o_s5p = self.dout("o_s5p", [L, 128, 2, 16]); self.o_s5s = self.dout("o_s5s", [L, 128, 2, 16, 16])
        self.c0s_d = din("c0s", [L, 16, 4, 256, 256])
        self.n0T_d = din("n0T", [128, L, 4, 2, 16]); self.m0T_d = din("m0T", [128, L, 16])
        self.conv0T_d = din("conv0T", [L, 128, 16, 16, 3])
        self.o_cp = self.dout("o_cp", [L, 4, 256, 256]); self.o_np = self.dout("o_np", [L, 128, 4, 2])
        self.o_mp = self.dout("o_mp", [L, 4, 1]); self.o_convp = self.dout("o_convp", [L, 128, 16, 3])
        self.o_cs = self.dout("o_cs", [L, 16, 4, 256, 256]); self.o_ns = self.dout("o_ns", [L, 128, 4, 2, 16])
        self.o_ms = self.dout("o_ms", [L, 4, 16]); self.o_convs = self.dout("o_convs", [L, 128, 16, 16, 3])

        self.T = TPP + NSAMP
        T = self.T
        self.xT = self.sb("xT", [128, 8, T], F32)
        self.gcols = self.sb("gcols_sb", [128, 4 * L + 1, 8], F32)
        self.cst = self.sb("cst", [128, CC_N], F32)
        self.ident = self.cst[:, 0:128]
        self.identb = self.sb("identb", [128, 128], BF16)[:]
        self.onesb = self.sb("onesb", [128, 128], BF16)[:]
        self.mixc = self.sb("mixc_sb", [128, L, MIXC_N], F32)
        self.s5car = self.sb("s5car", [128, L, 2, 16], F32)
        self.CST = self.sb("CST", [128, L * 4, 2, 257], F32)
        self.mcar = self.sb("mcar", [128, L], F32)
        self.convt = self.sb("convt", [128, L, 16, 3], F32)
        self.psum_init()
        rem = int(nc_rem(self.nc)) - 2048
        self.arena_init(rem // 128 * 128)
        self.ws_init(6)
        self.xn_off = self.mark()
        self.XN = self.alloc([8, T], BF16)
        self.a_persist = self.mark()

    def _body(self):
        self.dma("sp", self.gcols[:], self.gcols_d, "gcols", writes=[self.gcols[:]])
        self.dma("sp", self.cst[:], self.consts_d, "cst", writes=[self.cst[:]])
        self.dma("sp", self.mixc[:], self.mixc_d, "mixc", writes=[self.mixc[:]])
        self.copy("dve", self.identb[:], self.ident)
        self.memset("dve", self.onesb[:], 1.0)
        for ps in self.passes:
            self._pass(ps)

    def _pass(self, ps):
        T = self.T
        if ps == "A":
            self.ntok_p = TPP; self.ns = NSAMP; self.p0 = 0
        else:
            self.ntok_p = TPP; self.ns = 0; self.p0 = TPP
        self.Tc = self.ntok_p + self.ns
        self.tiles = [(0, 512), (512, 512)] + ([(1024, 64)] if self.ns else [])
        self.chunks = [(c * 128, 128) for c in range(self.ntok_p // 128)] + ([(TPP, 64)] if self.ns else [])
        self.cur_pass = ps
        self._load_x()
        for l in range(self.nl):
            self._layer(l)
        self._final()

    def _load_x(self):
        m = self.mark()
        stg = [self.alloc([D], F32) for _ in range(2)]
        for ci, (t0, tw) in enumerate(self.chunks):
            s = stg[ci % 2]
            src = self.x_p[self.p0 + t0: self.p0 + t0 + tw, :] if t0 < TPP else self.x_s[:, :]
            self.dma("sp", s[0:tw, :], src, ("xin", ci % 2), writes=[s[0:tw, :]])
            for half in range(2):
                b = self.bank()
                for q in range(4):
                    kc = half * 4 + q
                    self.transpose(b[:, q * 128: q * 128 + tw], s[0:tw, kc * 128:(kc + 1) * 128], self.ident[0:tw, 0:tw])
                src_ps = b[:, :].rearrange("p (q t) -> p q t", q=4)[:, :, 0:tw]
                self.copy("act" if half == 0 else "dve", self.xT[:, half * 4: half * 4 + 4, t0:t0 + tw], src_ps)
        self.release(m)

    def _norm(self, gi, out=None):
        out = self.XN if out is None else out
        m = self.mark()
        sq = self.alloc([8, 512], BF16)
        rs = self.alloc([512], F32)
        for (t0, tw) in self.tiles:
            self.act(sq[:, :, 0:tw], self.xT[:, :, t0:t0 + tw], AF.Square)
            ps = self.bank()[:, 0:tw]
            self.mm(ps, [(self.onesb[:], sq[:, kc, 0:tw]) for kc in range(8)])
            self.act(rs[:, 0:tw], ps, AF.Sqrt, bias=EPS, scale=1.0 / D)
            self.recip(rs[:, 0:tw], rs[:, 0:tw])
            for kc in range(8):
                self.stt(out[:, kc, t0:t0 + tw], self.xT[:, kc, t0:t0 + tw], self.gcols[:, gi, kc:kc + 1], rs[:, 0:tw], ALU.mult, ALU.mult)
        self.release(m)

    def _ffn(self, l, wg, wu, wd, gi):
        self._norm(gi)
        m = self.mark()
        HT = self.alloc([16, self.T], BF16)
        tmp = [self.alloc([512], F32) for _ in range(2)]
        XN = self.XN
        cnt = [0]

        def cons_up(j, ti, t0, tw, pss):
            t = tmp[cnt[0] % 2]; cnt[0] += 1
            self.act(t[:, 0:tw], pss[0], AF.Silu)
            self.tt("dve", HT[:, j, t0:t0 + tw], t[:, 0:tw], pss[1], ALU.mult)
        self.dense_multi([(wg[l], 0, 8, lambda kc, t0, tw: XN[:, kc, t0:t0 + tw]),
                          (wu[l], 0, 8, lambda kc, t0, tw: XN[:, kc, t0:t0 + tw])], DFF, cons_up)

        def cons_dn(j, ti, t0, tw, pss):
            self.stt(self.xT[:, j, t0:t0 + tw], pss[0], 0.5, self.xT[:, j, t0:t0 + tw], ALU.mult, ALU.add)
        self.dump("XN", self.XN, BF16); self.dump("HT", HT, BF16)
        self.dense_multi([(wd[l], 0, 16, lambda kc, t0, tw: HT[:, kc, t0:t0 + tw])], D, cons_dn)
        self.release(m)

    def _layer(self, l):
        self._ffn(l, self.w1_gate, self.w1_up, self.w1_down, 4 * l + 0)
        if self.stop == "ffn1":
            return
        self._mixer(l)
        if self.stop == "mixer":
            return
        self._ffn(l, self.w2_gate, self.w2_up, self.w2_down, 4 * l + 2)
        self._ple(l)

    def _ple(self, l):
        self._norm(4 * l + 3)
        m = self.mark()
        PT2 = self.alloc([2, self.T], BF16)
        stg = [self.alloc([DPLE], F32) for _ in range(2)]
        tmp = [self.alloc([512], F32) for _ in range(2)]
        for ci, (t0, tw) in enumerate(self.chunks):
            sg = stg[ci % 2]
            src = self.p_p[l, self.p0 + t0: self.p0 + t0 + tw, :] if t0 < TPP else self.p_s[l]
            self.dma("sp", sg[0:tw, :], src, ("pin", ci % 2), writes=[sg[0:tw, :]])
            b = self.bank()
            for kc in range(2):
                self.transpose(b[:, kc * 128: kc * 128 + tw], sg[0:tw, kc * 128:(kc + 1) * 128], self.ident[0:tw, 0:tw])
            self.copy("act", PT2[:, :, t0:t0 + tw], b[:, 0:256].rearrange("p (k t) -> p k t", k=2)[:, :, 0:tw])
        XN = self.XN
        cnt = [0]

        def cons(j, ti, t0, tw, pss):
            t = tmp[cnt[0] % 2]; cnt[0] += 1
            self.act(t[:, 0:tw], pss[0], AF.Sigmoid)
            self.tt("dve", t[:, 0:tw], t[:, 0:tw], pss[1], ALU.mult)
            self.tt("dve", self.xT[:, j, t0:t0 + tw], self.xT[:, j, t0:t0 + tw], t[:, 0:tw], ALU.add)
        self.dense_multi([(self.w_ple_gate[l], 0, 8, lambda kc, t0, tw: XN[:, kc, t0:t0 + tw]),
                          (self.w_ple[l], 0, 2, lambda kc, t0, tw: PT2[:, kc, t0:t0 + tw])], D, cons)
        self.release(m)

    def _rr_sin(self, out, x, xs, phase, tu, ti, tg):
        sc = 1.0 / TWO_PI
        self.ts("dve", tu, x, xs * sc, ALU.mult, phase * sc, ALU.add)
        self.copy("dve", ti, tu)
        self.copy("dve", tg, ti)
        self.tt("dve", tu, tu, tg, ALU.subtract)
        self.ts("dve", tg, tu, 0.5, ALU.is_gt)
        self.tt("dve", tu, tu, tg, ALU.subtract)
        self.ts("dve", tg, tu, -0.5, ALU.is_lt)
        self.tt("dve", tu, tu, tg, ALU.add)
        self.act(out, tu, AF.Sin, scale=TWO_PI * (1.0 - 1e-6))

    def _mixer(self, l):
        self._norm(4 * l + 1)
        mm0 = self.mark()
        T = self.T
        XN = self.XN
        UY = self.alloc([4, T], BF16)
        self.UY = UY

        def cons_u(j, ti, t0, tw, pss):
            self.copy("act", UY[:, j, t0:t0 + tw], pss[0])
        self.dense_multi([(self.w_in[l], C_U, 8, lambda kc, t0, tw: XN[:, kc, t0:t0 + tw])], 512, cons_u)
        if self.use_s5:
            self._s5(l)
            self._norm(4 * l + 1)
        if self.use_ml:
            self.HMT = self.alloc([8, T], BF16)
            self._mlstm(l)
        m = self.mark()
        MG = self.alloc([8, T], BF16)
        tmp = [self.alloc([512], F32) for _ in range(4)]
        cnt = [0]
        streams = []
        if self.use_s5:
            streams += [(self.w_in[l], C_GS5, 8, lambda kc, t0, tw: XN[:, kc, t0:t0 + tw]),
                        (self.w_s5_up[l], 0, 4, lambda kc, t0, tw: UY[:, kc, t0:t0 + tw])]
        if self.use_ml:
            HMT = self.HMT
            streams += [(self.w_in[l], C_GM, 8, lambda kc, t0, tw: XN[:, kc, t0:t0 + tw]),
                        (self.w_m_up[l], 0, 8, lambda kc, t0, tw: HMT[:, kc, t0:t0 + tw])]

        def cons_m(j, ti, t0, tw, pss):
            k = cnt[0] % 2; cnt[0] += 1
            ta = tmp[2 * k]; tb = tmp[2 * k + 1]
            self.act(ta[:, 0:tw], pss[0], AF.Sigmoid)
            if len(pss) == 2:
                self.tt("dve", MG[:, j, t0:t0 + tw], ta[:, 0:tw], pss[1], ALU.mult)
            else:
                self.tt("dve", ta[:, 0:tw], ta[:, 0:tw], pss[1], ALU.mult)
                self.act(tb[:, 0:tw], pss[2], AF.Sigmoid)
                self.tt("dve", tb[:, 0:tw], tb[:, 0:tw], pss[3], ALU.mult)
                self.tt("pool", MG[:, j, t0:t0 + tw], ta[:, 0:tw], tb[:, 0:tw], ALU.add)
        self.dense_multi(streams, D, cons_m)

        def cons_o(j, ti, t0, tw, pss):
            self.tt("dve", self.xT[:, j, t0:t0 + tw], self.xT[:, j, t0:t0 + tw], pss[0], ALU.add)
        self.dense_multi([(self.w_out[l], 0, 8, lambda kc, t0, tw: MG[:, kc, t0:t0 + tw])], D, cons_o)
        self.release(m)
        self.release(mm0)

    def _s5(self, l):
        m0 = self.mark()
        UY = self.UY
        A = ALU
        isA = self.cur_pass == "A"
        SM = self.alloc([40, 16], F32)
        sm = lambda i: SM[:, i, :]
        LRE, LIM, LDT, DT, R_, TH, CTH, STH, ARE, AIM, DEN, PR, WRE, WIM, T1, T2, C128, S128, TH128, TU, TG, FR, FI = [sm(i) for i in range(23)]
        SMI = self.alloc([16], I32)
        COS = self.alloc([16, 128], F32); SIN = self.alloc([16, 128], F32)
        BBT = [self.alloc([16, 128], BF16) for _ in range(2)]
        CT = [self.alloc([16, 128], BF16) for _ in range(2)]
        W = [self.view(self.xn_off, [16, 128], F32), self.view(self.xn_off + 8192, [16, 128], F32),
             self.alloc([16, 128], F32), self.alloc([16, 128], F32)]
        WI = W[3].bitcast(I32)
        HB = [self.alloc([16, 128], BF16) for _ in range(2)]
        YT = self.alloc([4, 128], F32); YG = self.alloc([4, 128], BF16); SG = self.alloc([4, 128], BF16)
        INIT = self.s5car[:, l]
        sp = self.alloc([3, 16], F32)
        self.dma("sp", sp, self.s5p_d[l], "s5p", writes=[sp])
        self.copy("dve", LRE, sp[:, 0, :]); self.copy("dve", LIM, sp[:, 1, :])
        self.act(DT, sp[:, 2, :], AF.Exp)
        self.tt("dve", T1, LRE, DT, A.mult)
        self.act(R_, T1, AF.Exp)
        self.tt("dve", TH, LIM, DT, A.mult)
        self.act(T2, TH, AF.Abs)
        self._rr_sin(STH, T2, 1.0, 0.0, TU, SMI, TG)
        self._rr_sin(CTH, T2, 1.0, math.pi / 2, TU, SMI, TG)
        self.act(T1, TH, AF.Sign)
        self.tt("dve", STH, STH, T1, A.mult)
        self.tt("dve", ARE, R_, CTH, A.mult); self.tt("dve", AIM, R_, STH, A.mult)
        self.tt("dve", DEN, LRE, LRE, A.mult); self.tt("dve", T1, LIM, LIM, A.mult)
        self.tt("dve", DEN, DEN, T1, A.add); self.recip(DEN, DEN)
        self.ts("dve", PR, ARE, -1.0, A.add)
        self.tt("dve", WRE, PR, LRE, A.mult); self.tt("dve", T1, AIM, LIM, A.mult)
        self.tt("dve", WRE, WRE, T1, A.add); self.tt("dve", WRE, WRE, DEN, A.mult)
        self.tt("dve", WIM, AIM, LRE, A.mult); self.tt("dve", T1, PR, LIM, A.mult)
        self.tt("dve", WIM, WIM, T1, A.subtract); self.tt("dve", WIM, WIM, DEN, A.mult)
        jrow = self.cst[:, CC_J:CC_J + 128]
        ANG = W[0]
        self.tt("dve", ANG, T2.unsqueeze(2).broadcast_to([128, 16, 128]), jrow.unsqueeze(1).broadcast_to([128, 16, 128]), A.mult)
        f2 = lambda a: a.rearrange("p a b -> p (a b)")
        self._rr_sin(f2(SIN), f2(ANG), 1.0, 0.0, f2(W[1]), f2(WI), f2(W[2]))
        self._rr_sin(f2(COS), f2(ANG), 1.0, math.pi / 2, f2(W[1]), f2(WI), f2(W[2]))
        self.act(T1, TH, AF.Sign)
        self.tt("dve", SIN, SIN, T1.unsqueeze(2).broadcast_to([128, 16, 128]), A.mult)
        self.ts("dve", TH128, T2, 128.0, A.mult)
        self._rr_sin(S128, TH128, 1.0, 0.0, TU, SMI, TG)
        self._rr_sin(C128, TH128, 1.0, math.pi / 2, TU, SMI, TG)
        self.tt("dve", S128, S128, T1, A.mult)
        XB = [W[0], W[1]]
        self.dma("sp", XB[0], self.s5B_d[l, 0], "s5x0", writes=[XB[0]])
        self.dma("sp", XB[1], self.s5B_d[l, 1], "s5x1", writes=[XB[1]])
        bc = lambda a: a.unsqueeze(2).broadcast_to([128, 16, 128])
        self.tt("dve", W[2], XB[0], bc(WRE), A.mult); self.tt("pool", W[3], XB[1], bc(WIM), A.mult)
        self.tt("dve", W[2], W[2], W[3], A.subtract)
        self.tt("pool", W[3], XB[1], bc(WRE), A.mult); self.tt("dve", XB[1], XB[0], bc(WIM), A.mult)
        self.tt("dve", W[3], W[3], XB[1], A.add)
        for part in range(2):
            for q in range(4):
                b = self.bank()
                for gi in range(4):
                    self.transpose(b[:, gi * 128:(gi + 1) * 128], W[2 + part][:, 4 * q + gi, :], self.ident)
                self.copy("act" if q % 2 == 0 else "dve", BBT[part][:, 4 * q:4 * q + 4, :], b[:, :].rearrange("p (a b) -> p a b", a=4))
        self.dma("sp", W[0], self.s5C_d[l, 0], "s5x0", writes=[W[0]])
        self.dma("sp", W[1], self.s5C_d[l, 1], "s5x1", writes=[W[1]])
        self.copy("dve", CT[0], W[0])
        self.act(CT[1], W[1], AF.Copy, scale=-1.0)
        WG = self.wget(self.s5_w_glu[l], 0, 512, 4)
        Dbc = self.mixc[:, l, MC_D:MC_D + 4]
        if isA:
            self.memset("dve", INIT, 0.0)

        def y_and_glu(t0, tw, HBv):
            Y = self.bank()
            for fc in range(4):
                prs = []
                for gc in range(4 * fc, 4 * fc + 4):
                    prs.append((CT[0][:, gc, :], HBv[0][:, gc, 0:tw])); prs.append((CT[1][:, gc, :], HBv[1][:, gc, 0:tw]))
                self.mm(Y[:, fc * tw:(fc + 1) * tw], prs)
            Y3 = Y[:, 0:4 * tw].rearrange("p (a b) -> p a b", a=4)
            self.tt("pool", YT[:, :, 0:tw], UY[:, :, t0:t0 + tw], Dbc.unsqueeze(2).broadcast_to([128, 4, tw]), A.mult)
            self.tt("dve", YT[:, :, 0:tw], YT[:, :, 0:tw], Y3, A.add)
            self.act(YG[:, :, 0:tw], YT[:, :, 0:tw], AF.Gelu)
            PG = self.bank()
            for oc in range(4):
                self.mm(PG[:, oc * tw:(oc + 1) * tw], [(WG[:, kc, oc * 128:(oc + 1) * 128], YG[:, kc, 0:tw]) for kc in range(4)])
            self.act(SG[:, :, 0:tw], PG[:, 0:4 * tw].rearrange("p (a b) -> p a b", a=4), AF.Sigmoid)
            self.tt("dve", UY[:, :, t0:t0 + tw], YG[:, :, 0:tw], SG[:, :, 0:tw], A.mult)

        nch = self.ntok_p // 128
        for ci in range(nch):
            t0 = ci * 128
            for q in range(4):
                bre = self.bank(); bim = self.bank()
                for gi in range(4):
                    gc = 4 * q + gi
                    self.mm(bre[:, gi * 128:(gi + 1) * 128], [(BBT[0][:, gc, :], UY[:, q, t0:t0 + 128])])
                    self.mm(bim[:, gi * 128:(gi + 1) * 128], [(BBT[1][:, gc, :], UY[:, q, t0:t0 + 128])])
                r3 = bre[:, :].rearrange("p (a b) -> p a b", a=4); i3 = bim[:, :].rearrange("p (a b) -> p a b", a=4)
                qs = slice(4 * q, 4 * q + 4)
                ae = "dve" if q == 3 else "pool"
                self.tt("dve", W[0][:, qs, :], r3, COS[:, qs, :], A.mult)
                self.tt("dve", W[2][:, qs, :], i3, SIN[:, qs, :], A.mult)
                self.tt(ae, W[0][:, qs, :], W[0][:, qs, :], W[2][:, qs, :], A.add)
                self.tt("dve", W[1][:, qs, :], i3, COS[:, qs, :], A.mult)
                self.tt("dve", W[3][:, qs, :], r3, SIN[:, qs, :], A.mult)
                self.tt(ae, W[1][:, qs, :], W[1][:, qs, :], W[3][:, qs, :], A.subtract)
            for gc in range(16):
                for part in range(2):
                    self.scan(W[2 + part][:, gc, :], R_[:, gc:gc + 1].broadcast_to([128, 128]), W[part][:, gc, :],
                              INIT[:, part, gc:gc + 1], A.mult, A.add)
            self.copy("dve", FR, W[2][:, :, 127]); self.copy("dve", FI, W[3][:, :, 127])
            self.tt("dve", T1, FR, C128, A.mult); self.tt("dve", TU, FI, S128, A.mult)
            self.tt("dve", INIT[:, 0, :], T1, TU, A.subtract)
            self.tt("dve", T1, FR, S128, A.mult); self.tt("dve", TU, FI, C128, A.mult)
            self.tt("dve", INIT[:, 1, :], T1, TU, A.add)
            lo = slice(0, 8); hi = slice(8, 16)
            self.tt("pool", W[1][:, hi], W[3][:, hi], SIN[:, hi], A.mult)
            self.tt("dve", W[0], W[2], COS, A.mult); self.tt("dve", W[1][:, lo], W[3][:, lo], SIN[:, lo], A.mult)
            self.tt("dve", HB[0], W[0], W[1], A.subtract)
            self.tt("pool", W[1][:, hi], W[3][:, hi], COS[:, hi], A.mult)
            self.tt("dve", W[0], W[2], SIN, A.mult); self.tt("dve", W[1][:, lo], W[3][:, lo], COS[:, lo], A.mult)
            self.tt("dve", HB[1], W[0], W[1], A.add)
            y_and_glu(t0, 128, HB)
        if not isA:
            OS = self.alloc([2, 16], F32)
            self.tt("dve", T1, FR, COS[:, :, 127], A.mult); self.tt("dve", TU, FI, SIN[:, :, 127], A.mult)
            self.tt("dve", OS[:, 0, :], T1, TU, A.subtract)
            self.tt("dve", T1, FR, SIN[:, :, 127], A.mult); self.tt("dve", TU, FI, COS[:, :, 127], A.mult)
            self.tt("dve", OS[:, 1, :], T1, TU, A.add)
            self.dma("sp", self.o_s5p[l], OS, "os5p", reads=[OS])
        else:
            t0 = TPP
            H0 = self.alloc([2, 16, 16], F32)
            self.dma("sp", H0, self.s5h0_d[l].rearrange("r p a b -> p r a b"), "s5h0", writes=[H0])
            X0 = self.alloc([2, 16, 16], F32)
            TS = self.alloc([16, 16], F32)
            b16 = lambda a: a.unsqueeze(2).broadcast_to([128, 16, 16])
            self.tt("dve", X0[:, 0], H0[:, 0], b16(ARE), A.mult); self.tt("dve", TS, H0[:, 1], b16(AIM), A.mult)
            self.tt("dve", X0[:, 0], X0[:, 0], TS, A.subtract)
            self.tt("dve", X0[:, 1], H0[:, 1], b16(ARE), A.mult); self.tt("dve", TS, H0[:, 0], b16(AIM), A.mult)
            self.tt("dve", X0[:, 1], X0[:, 1], TS, A.add)
            RM = self.alloc([16, 64], F32)
            m4 = self.cst[:, CC_M4:CC_M4 + 64]
            self.tt("dve", RM, R_.unsqueeze(2).broadcast_to([128, 16, 64]), m4.unsqueeze(1).broadcast_to([128, 16, 64]), A.mult)
            w4 = lambda a, qs: a[:, qs, 0:64].rearrange("p a (s j) -> p a s j", j=4)
            for q in range(4):
                bre = self.bank(); bim = self.bank()
                for gi in range(4):
                    gc = 4 * q + gi
                    self.mm(bre[:, gi * 64:(gi + 1) * 64], [(BBT[0][:, gc, :], UY[:, q, t0:t0 + 64])])
                    self.mm(bim[:, gi * 64:(gi + 1) * 64], [(BBT[1][:, gc, :], UY[:, q, t0:t0 + 64])])
                qs = slice(4 * q, 4 * q + 4)
                r4 = bre[:, 0:256].rearrange("p (a s j) -> p a s j", a=4, j=4); i4 = bim[:, 0:256].rearrange("p (a s j) -> p a s j", a=4, j=4)
                c4 = COS[:, qs, 0:4].unsqueeze(2).broadcast_to([128, 4, 16, 4]); s4 = SIN[:, qs, 0:4].unsqueeze(2).broadcast_to([128, 4, 16, 4])
                self.tt("dve", w4(W[0], qs), r4, c4, A.mult)
                self.tt("dve", w4(W[2], qs), i4, s4, A.mult)
                self.tt("pool", w4(W[0], qs), w4(W[0], qs), w4(W[2], qs), A.add)
                self.tt("dve", w4(W[1], qs), i4, c4, A.mult)
                self.tt("dve", w4(W[3], qs), r4, s4, A.mult)
                self.tt("pool", w4(W[1], qs), w4(W[1], qs), w4(W[3], qs), A.subtract)
            for part in range(2):
                v0 = W[part][:, :, 0:64].rearrange("p a (s j) -> p a s j", j=4)[:, :, :, 0]
                self.tt("dve", v0, v0, X0[:, part], A.add)
            for gc in range(16):
                for part in range(2):
                    self.scan(W[2 + part][:, gc, 0:64], RM[:, gc, :], W[part][:, gc, 0:64], 0.0, A.mult, A.add)
            al = slice(0, 16)
            g4r = W[2][:, :, 0:64].rearrange("p a (s j) -> p a s j", j=4); g4i = W[3][:, :, 0:64].rearrange("p a (s j) -> p a s j", j=4)
            ca = COS[:, :, 0:4].unsqueeze(2).broadcast_to([128, 16, 16, 4]); sa = SIN[:, :, 0:4].unsqueeze(2).broadcast_to([128, 16, 16, 4])
            self.tt("dve", w4(W[0], al), g4r, ca, A.mult); self.tt("pool", w4(W[1], al), g4i, sa, A.mult)
            OSS = self.alloc([2, 16, 16], F32)
            j3 = lambda a: a[:, :, 0:64].rearrange("p a (s j) -> p a s j", j=4)[:, :, :, 3]
            self.tt("dve", HB[0][:, :, 0:64], W[0][:, :, 0:64], W[1][:, :, 0:64], A.subtract)
            self.tt("dve", OSS[:, 0], j3(W[0]), j3(W[1]), A.subtract)
            self.tt("dve", w4(W[0], al), g4r, sa, A.mult); self.tt("pool", w4(W[1], al), g4i, ca, A.mult)
            self.tt("dve", HB[1][:, :, 0:64], W[0][:, :, 0:64], W[1][:, :, 0:64], A.add)
            self.tt("dve", OSS[:, 1], j3(W[0]), j3(W[1]), A.add)
            self.dma("sp", self.o_s5s[l], OSS, "os5s", reads=[OSS])
            y_and_glu(t0, 64, HB)
        self.release(m0)

    def _mnorm(self, TOT, e2_col, P, HN, sm, SQ):
        A = ALU
        S257, Tm, Av, Vv, Sc = [sm[0:P, i:i + 1] for i in range(5)]
        self.tt("dve", SQ[0:P, :], TOT[0:P, :], TOT[0:P, :], A.mult)
        self.op("dve", lambda e: e.reduce_sum(out=S257, in_=SQ[0:P, :], axis=mybir.AxisListType.X), reads=[SQ[0:P, :]], writes=[S257])
        self.ts("dve", Tm, SQ[0:P, 256:257], e2_col, A.max)
        self.tt("dve", Av, S257, SQ[0:P, 256:257], A.subtract)
        self.stt(Vv, Tm, EPS * float(DH), Av, A.mult, A.add)
        self.act(Vv, Vv, AF.Ln, scale=1.0 / DH)
        self.act(Sc, Vv, AF.Exp, scale=-0.5)
        self.act(HN[0:P, :], TOT[0:P, 0:256], AF.Copy, scale=Sc)

    def _mlstm(self, l):
        m0 = self.mark()
        A = ALU
        isA = self.cur_pass == "A"
        T = self.T; Tc = self.Tc; NP = self.ntok_p
        XN = self.XN; HMT = self.HMT; cst = self.cst; mixc = self.mixc
        nchp = NP // 128
        rowM = self.alloc([T], F32)
        M = rowM[0:4, :]
        SMALL = self.alloc([128], F32)
        MP = SMALL[0:4, 0:24]; ML = SMALL[0:4, 24:48]; BL = SMALL[0:4, 48:72]; DEC = SMALL[0:4, 72:96]; MNEW = SMALL[0:4, 96:120]
        MS0 = self.alloc([16], F32)
        COLS = self.alloc([9, 16], F32)
        DECB = self.alloc([4, 24], F32)
        oh = lambda hd, P=128: cst[0:4, CC_OH + 128 * hd: CC_OH + 128 * hd + P]
        mrow = self.mark()
        rows = [self.alloc([T], F32) for _ in range(4)] + [rowM]
        LI, LF, B, D0 = [r[0:4, :] for r in rows[0:4]]
        X1, X2 = [self.alloc([T], F32)[0:4, :] for _ in range(2)]
        self.ws_live = 1
        WGt = self.wget(self.w_in[l], C_IG, 8, 8)
        for (t0, tw) in self.tiles:
            pi = self.bank()[0:4, 0:tw]; pf = self.bank()[0:4, 0:tw]
            self.mm(pi, [(WGt[:, kc, 0:4], XN[:, kc, t0:t0 + tw]) for kc in range(8)])
            self.mm(pf, [(WGt[:, kc, 4:8], XN[:, kc, t0:t0 + tw]) for kc in range(8)])
            self.act(LI[:, t0:t0 + tw], pi, AF.Identity, bias=mixc[0:4, l, MC_BI:MC_BI + 1])
            self.act(D0[:, t0:t0 + tw], pf, AF.Identity, bias=mixc[0:4, l, MC_BF:MC_BF + 1])
        al = slice(0, Tc)
        self.act(X1[:, al], D0[:, al], AF.Abs)
        self.act(X1[:, al], X1[:, al], AF.Exp, scale=-1.0)
        self.act(X1[:, al], X1[:, al], AF.Ln, bias=1.0)
        self.ts("dve", X2[:, al], D0[:, al], 0.0, A.min)
        self.tt("dve", LF[:, al], X2[:, al], X1[:, al], A.subtract)
        self.scan(B[:, al], cst[0:4, CC_M01:CC_M01 + Tc], LF[:, al], 0.0, A.mult, A.add)
        self.tt("dve", LI[:, al], LI[:, al], B[:, al], A.subtract)
        C_ = LI
        p3 = lambda a: a[:, 0:NP].rearrange("p (c j) -> p c j", j=128)
        s4 = lambda a: a[:, NP:NP + NSAMP].rearrange("p (s j) -> p s j", j=4)
        if isA:
            self.memset("dve", self.mcar[0:4, l:l + 1], 0.0)
        self.memset("dve", D0[:, 0:NP], 0.0)
        self.copy("dve", p3(D0)[:, 1:nchp, 0], p3(B)[:, 0:nchp - 1, 127])
        self.scan(M[:, 0:NP], D0[:, 0:NP], C_[:, 0:NP], self.mcar[0:4, l:l + 1], A.add, A.max)
        self.copy("dve", MP[:, 0:1], self.mcar[0:4, l:l + 1])
        self.tt("dve", MP[:, 1:nchp], p3(B)[:, 0:nchp - 1, 127], p3(M)[:, 0:nchp - 1, 127], A.add)
        self.copy("dve", ML[:, 0:nchp], p3(M)[:, :, 127]); self.copy("dve", BL[:, 0:nchp], p3(B)[:, :, 127])
        ncol = nchp
        if isA:
            self.dma("sp", MS0[0:4, :], self.m0T_d[0:4, l, :], "ms0", writes=[MS0[0:4, :]])
            self.tt("dve", s4(M)[:, :, 0], MS0[0:4, :], s4(C_)[:, :, 0], A.max)
            for j in range(1, 4):
                self.tt("dve", s4(M)[:, :, j], s4(M)[:, :, j - 1], s4(C_)[:, :, j], A.max)
            self.copy("dve", MP[:, 8:24], MS0[0:4, :])
            self.copy("dve", ML[:, 8:24], s4(M)[:, :, 3]); self.copy("dve", BL[:, 8:24], s4(B)[:, :, 3])
            ncol = 24
        self.tt("dve", DEC[:, 0:ncol], MP[:, 0:ncol], ML[:, 0:ncol], A.subtract)
        self.act(DEC[:, 0:ncol], DEC[:, 0:ncol], AF.Exp)
        self.tt("dve", MNEW[:, 0:ncol], BL[:, 0:ncol], ML[:, 0:ncol], A.add)
        self.copy("dve", self.mcar[0:4, l:l + 1], MNEW[:, nchp - 1:nchp])
        self.tt("dve", p3(D0), MP[:, 0:nchp].unsqueeze(2).broadcast_to([4, nchp, 128]), p3(M), A.subtract)
        self.tt("dve", p3(X2), p3(C_), ML[:, 0:nchp].unsqueeze(2).broadcast_to([4, nchp, 128]), A.subtract)
        if isA:
            self.tt("dve", s4(D0), MP[:, 8:24].unsqueeze(2).broadcast_to([4, 16, 4]), s4(M), A.subtract)
            self.tt("dve", s4(X2), s4(C_), ML[:, 8:24].unsqueeze(2).broadcast_to([4, 16, 4]), A.subtract)
        self.act(D0[:, al], D0[:, al], AF.Exp, bias=-LN16)
        self.act(LF[:, al], X2[:, al], AF.Exp)
        self.tt("dve", X1[:, al], B[:, al], M[:, al], A.add)
        self.act(B[:, al], X1[:, al], AF.Exp, scale=-2.0)
        A_ = D0; E_r = B; WK = LF
        cb = self.bank()
        if self.cfg.get("pe_probe"):
            dmy = self.bank()
            for _ in range(self.cfg["pe_probe"]):
                self.mm(dmy[0:128, 0:16], [(C_[:, 0:128], cst[0:4, CC_SEL: CC_SEL + 16])])
        for ci, (t0, tw) in enumerate(self.chunks):
            self.mm(cb[0:tw, ci * 16: ci * 16 + 16],
                    [(rt[:, t0:t0 + tw], cst[0:4, CC_SEL + 16 * qi: CC_SEL + 16 * qi + 16]) for qi, rt in enumerate([C_, A_, E_r, WK])])
        nck = len(self.chunks)
        self.copy("dve", COLS[:, 0:nck, :], cb[:, 0:nck * 16].rearrange("p (c q) -> p c q", q=16))
        if self.cfg.get("dump"):
            CBD = self.alloc([144], F32)
            self.copy("act", CBD, cb[:, 0:144])
            self.dump("CBD", CBD)
        self.dump("COLS_a", COLS)
        col = lambda ci, qi, hd, P=128: COLS[0:P, ci, qi * 4 + hd: qi * 4 + hd + 1]
        db = self.bank()
        for hd in range(4):
            self.mm(db[:, hd * 32: hd * 32 + ncol], [(oh(hd), DEC[:, 0:ncol])])
        self.copy("dve", DECB[:, :, 0:ncol], db[:, 0:128].rearrange("p (h c) -> p h c", h=4)[:, :, 0:ncol])
        self.dump("COLS", COLS); self.dump("rowC", rows[0]); self.dump("rowWK", rows[1]); self.dump("rowE", rows[2]); self.dump("rowA", rows[3]); self.dump("rowM", rows[4]); self.dump("DECB", DECB)
        self.dump("COLS_b", COLS)
        self.release(mrow)
        QT = self.alloc([2, T], BF16); KT = self.alloc([2, T], BF16)
        QS32 = self.alloc([2, 64], F32)
        VT = self.alloc([9, 257], BF16)
        RAW = self.alloc([515], F32); RAWS = self.alloc([16, 7], F32); ACC = self.alloc([512], F32)
        CnB = self.alloc([2, 257], BF16)
        ARG = self.alloc([128], F32); Wt = self.alloc([128], F32); PT = self.alloc([128], BF16)
        T2 = self.alloc([257], F32); TOT = self.alloc([257], F32); HN = self.alloc([256], BF16)
        KW2 = [self.alloc([256], BF16) for _ in range(2)]; KWS = self.alloc([256], BF16)
        smn = self.alloc([8], F32)
        N1SB = [self.alloc([257], F32) for _ in range(2)]
        if isA:
            CS = [self.alloc([2, 257], F32) for _ in range(2)]; CO = [self.alloc([2, 257], F32) for _ in range(2)]
            VBD = [self.alloc([258], BF16) for _ in range(2)]
            N2ACC = self.alloc([257], F32); N1S = self.alloc([257], F32)
            N0T = self.alloc([4, 2, 16], F32); NOUT = self.alloc([4, 2, 16], F32)
            self.dma("sp", N0T, self.n0T_d[:, l], "n0t", writes=[N0T])
        else:
            NOUTP = self.alloc([4, 2], F32)
        self.memset("dve", VT[:, :, 256:257], 1.0)
        bigm = cst[:, CC_BM:CC_BM + 128]; bigms = cst[0:64, CC_BMS:CC_BMS + 64]
        cw = lambda ch, j: mixc[:, l, MC_CW + ch * 4 + j: MC_CW + ch * 4 + j + 1]
        cbias = lambda ch: mixc[:, l, MC_CB + ch: MC_CB + ch + 1]
        ntl = len(self.tiles)
        evq = [0]
        for hd in range(4):
            for which, DST in ((0, QT), (1, KT)):
                c0 = C_QK + which * 1024 + hd * 256

                def cons_qk(j, ti, t0, tw, pss, which=which, DST=DST):
                    ch = which * 8 + hd * 2 + j
                    if t0 < NP:
                        if ti == 0:
                            if isA:
                                self.memset("dve", RAW[:, 0:3], 0.0)
                            else:
                                self.copy("dve", RAW[:, 0:3], self.convt[:, l, ch, :])
                        self.copy("act", RAW[:, 3:3 + tw], pss[0])
                        self.ts("dve", ACC[:, 0:tw], RAW[:, 0:tw], cw(ch, 0), A.mult, cbias(ch), A.add)
                        for jj in range(1, 4):
                            self.stt(ACC[:, 0:tw], RAW[:, jj:jj + tw], cw(ch, jj), ACC[:, 0:tw], A.mult, A.add)
                        self.act(DST[:, j, t0:t0 + tw], ACC[:, 0:tw], AF.Silu)
                        if t0 + tw < NP:
                            self.copy("dve", RAW[:, 0:3], RAW[:, tw:tw + 3])
                        else:
                            self.copy("dve", self.convt[:, l, ch, :], RAW[:, tw:tw + 3])
                    else:
                        self.dma("sp", RAWS[:, :, 0:3], self.conv0T_d[l, :, ch], "raws", writes=[RAWS[:, :, 0:3]])
                        self.copy("act", RAWS[:, :, 3:7], pss[0].rearrange("p (s j) -> p s j", j=4))
                        a3 = ACC[:, 0:64].rearrange("p (s j) -> p s j", j=4)
                        self.ts("dve", a3, RAWS[:, :, 0:4], cw(ch, 0), A.mult, cbias(ch), A.add)
                        for jj in range(1, 4):
                            self.stt(a3, RAWS[:, :, jj:jj + 4], cw(ch, jj), a3, A.mult, A.add)
                        self.act(DST[:, j, t0:t0 + tw], ACC[:, 0:64], AF.Silu)
                        if which == 0:
                            self.act(QS32[:, j, :], ACC[:, 0:64], AF.Silu)
                        self.dma("sp", self.o_convs[l, :, ch], RAWS[:, :, 4:7], "oconvs", reads=[RAWS[:, :, 4:7]])
                self.dense_multi([(self.w_in[l], c0, 8, lambda kc, t0, tw: XN[:, kc, t0:t0 + tw])], 256, cons_qk)
            self.dump("COLS_c%d" % hd, COLS)
            self.ws_live = 1
            WV = self.wget(self.w_in[l], C_V + hd * 256, 256, 8)
            for ci, (t0, tw) in enumerate(self.chunks):
                pv = self.bank()[0:tw, 0:256]
                self.mm(pv, [(XN[:, kc, t0:t0 + tw], WV[:, kc, :]) for kc in range(8)])
                self.copy("act" if ci % 2 == 0 else "dve", VT[0:tw, ci, 0:256], pv)
            self.dump("COLS_d%d" % hd, COLS)
            Cn = self.CST[:, l * 4 + hd]
            if isA:
                self.memset("dve", Cn, 0.0)
            self.copy("dve", CnB, Cn)

            N1B = [self.banks[5], self.banks[6]]

            def rb():
                b = self.banks[self.bi % 5]
                self.bi += 1
                return b

            def stage1(ci, t0, P, bm, kw_out, n1_bank, n1_sb=None):
                ST = rb()[0:P, 0:P]
                self.mm(ST, [(KT[:, dc, t0:t0 + P], QT[:, dc, t0:t0 + P]) for dc in range(2)])
                EB = rb()[0:P, 0:P]
                self.mm(EB, [(oh(hd, P), M[:, t0:t0 + P]), (self.ident[0:P, 0:P], bm)])
                self.act(Wt[0:P, 0:P], EB, AF.Exp, scale=-1.0, bias=col(ci, 0, hd, P))
                self.tt("dve", PT[0:P, 0:P], ST, Wt[0:P, 0:P], A.mult)
                N1 = n1_bank[0:P, 0:257]
                self.mm(N1, [(PT[0:P, 0:P], VT[0:P, ci, :])])
                kb = rb()[:, 0:128].bitcast(BF16)
                self.transposes([(kb[0:P, dc * 128:(dc + 1) * 128], KT[:, dc, t0:t0 + P], self.identb) for dc in range(2)])
                self.act(kw_out[0:P, :], kb[0:P, :], AF.Copy, scale=col(ci, 3, hd, P))
                if n1_sb is not None:
                    self.copy("act", n1_sb[0:P, :], N1)
                    return n1_sb[0:P, :]
                return N1

            def finish(ci, t0, P, N1, N2src):
                self.stt(TOT[0:P, :], N2src, col(ci, 1, hd, P), N1, A.mult, A.add)
                self._mnorm(TOT, col(ci, 2, hd, P), P, HN, smn, T2)
                hb = rb()[:, 0:128].bitcast(BF16)
                self.transposes([(hb[:, dc * 128: dc * 128 + P], HN[0:P, dc * 128:(dc + 1) * 128], self.identb[0:P, 0:P]) for dc in range(2)])
                self.copy("act", HMT[:, hd * 2: hd * 2 + 2, t0:t0 + P], hb[:, :].rearrange("p (d t) -> p d t", d=2)[:, :, 0:P])

            def stage2(ci, t0, N1, kw):
                N2 = rb()[:, 0:257]
                self.mm(N2, [(QT[:, kc, t0:t0 + 128], CnB[:, kc, :]) for kc in range(2)])
                finish(ci, t0, 128, N1, N2)
                for kc in range(2):
                    U = rb()[:, 0:257]
                    self.mm(U, [(kw[:, kc * 128:(kc + 1) * 128], VT[:, ci, :])])
                    self.stt(Cn[:, kc, :], Cn[:, kc, :], DECB[:, hd, ci:ci + 1], U, A.mult, A.add)
                self.copy("act", CnB, Cn)

            def sample_seq(sq):
                ci = nchp
                cs = CS[sq % 2]; co = CO[sq % 2]
                self.dma("sp", cs[:, :, 0:256], self.c0s_d[l, sq, hd].rearrange("(kc p) v -> p kc v", p=128), ("csin", sq % 2), writes=[cs[:, :, 0:256]])
                self.copy("dve", cs[:, :, 256], N0T[:, hd, :, sq])
                vbd = VBD[sq % 2]
                self.act(vbd[0:64, 0:257], VT[0:64, ci, :], AF.Copy, scale=cst[0:64, CC_BLK + sq: CC_BLK + sq + 1])
                N2s = rb()[0:64, 0:257]
                self.mm(N2s, [(QS32[:, kc, :], cs[:, kc, :]) for kc in range(2)])
                self.stt(N2ACC[0:64, :], N2s, cst[0:64, CC_BLK + sq: CC_BLK + sq + 1], N2ACC[0:64, :], A.mult, A.add)
                for kc in range(2):
                    U = rb()[:, 0:257]
                    self.mm(U, [(KWS[0:64, kc * 128:(kc + 1) * 128], vbd[0:64, 0:257])])
                    self.stt(co[:, kc, :], cs[:, kc, :], DECB[:, hd, 8 + sq: 9 + sq], U, A.mult, A.add)
                self.dma("sp", self.o_cs[l, sq, hd].rearrange("(kc p) v -> p kc v", p=128), co[:, :, 0:256], ("csout", sq % 2), reads=[co[:, :, 0:256]])
                self.copy("dve", NOUT[:, hd, :, sq], co[:, :, 256])

            if isA:
                N1p = stage1(nchp, NP, 64, bigms, KWS, N1B[0])
                self.copy("act", N1S[0:64, :], N1p)
                self.memset("dve", N2ACC[0:64, :], 0.0)
            n1s = [None, None]
            n1s[0] = stage1(0, 0, 128, bigm, KW2[0], N1B[0], N1SB[0])
            for ci in range(nchp):
                if ci + 1 < nchp:
                    n1s[(ci + 1) % 2] = stage1(ci + 1, (ci + 1) * 128, 128, bigm, KW2[(ci + 1) % 2], N1B[(ci + 1) % 2], N1SB[(ci + 1) % 2])
                stage2(ci, ci * 128, n1s[ci % 2], KW2[ci % 2])
                if isA:
                    for sq in range(2 * ci, 2 * ci + 2):
                        sample_seq(sq)
            if not isA:
                self.dma("sp", self.o_cp[l, hd].rearrange("(kc p) v -> p kc v", p=128), Cn[:, :, 0:256], "ocp", reads=[Cn[:, :, 0:256]])
                self.copy("dve", NOUTP[:, hd, :], Cn[:, :, 256])
            else:
                finish(nchp, NP, 64, N1S[0:64, :], N2ACC[0:64, :])
        accb = ACC.bitcast(BF16)
        sgt = [accb[:, 0:512], accb[:, 512:1024]]
        cnt = [0]

        def cons_o(j, ti, t0, tw, pss):
            t = sgt[cnt[0] % 2]; cnt[0] += 1
            self.act(t[:, 0:tw], pss[0], AF.Sigmoid)
            self.stt(HMT[:, j, t0:t0 + tw], HMT[:, j, t0:t0 + tw], mixc[:, l, MC_GMH + j: MC_GMH + j + 1], t[:, 0:tw], A.mult, A.mult)
        self.dense_multi([(self.w_in[l], C_O, 8, lambda kc, t0, tw: XN[:, kc, t0:t0 + tw])], D, cons_o)
        self.dump("HMT", HMT, BF16); self.dump("CST", self.CST[:, l * 4: l * 4 + 4].rearrange("p a b c -> p (a b c)")); self.dump("QT", QT, BF16); self.dump("KT", KT, BF16); self.dump("VT", VT, BF16)
        if isA:
            self.dma("sp", self.o_ns[l], NOUT, "ons", reads=[NOUT])
            self.dma("sp", self.o_ms[l], MNEW[:, 8:24], "oms", reads=[MNEW[:, 8:24]])
        else:
            self.dma("sp", self.o_np[l], NOUTP, "onp", reads=[NOUTP])
            self.dma("sp", self.o_mp[l], MNEW[:, nchp - 1:nchp], "omp", reads=[MNEW[:, nchp - 1:nchp]])
            self.dma("sp", self.o_convp[l], self.convt[:, l], "oconvp", reads=[self.convt[:, l]])
        self.release(m0)

    def _final(self):
        m = self.mark()
        XF = self.alloc([8, 128], F32)
        sq = self.alloc([8, 128], BF16)
        rs = self.alloc([128], F32)
        yo = [self.alloc([D], F32) for _ in range(2)]
        gi = 4 * NL
        for ci, (t0, tw) in enumerate(self.chunks):
            self.act(sq[:, :, 0:tw], self.xT[:, :, t0:t0 + tw], AF.Square)
            ps = self.bank()[:, 0:tw]
            self.mm(ps, [(self.onesb[:], sq[:, kc, 0:tw]) for kc in range(8)])
            self.act(rs[:, 0:tw], ps, AF.Sqrt, bias=EPS, scale=1.0 / D)
            self.recip(rs[:, 0:tw], rs[:, 0:tw])
            for kc in range(8):
                self.stt(XF[:, kc, 0:tw], self.xT[:, kc, t0:t0 + tw], self.gcols[:, gi, kc:kc + 1], rs[:, 0:tw], ALU.mult, ALU.mult)
            y = yo[ci % 2]
            for half in range(2):
                b = self.bank()
                for q in range(4):
                    kc = half * 4 + q
                    self.transpose(b[0:tw, q * 128:(q + 1) * 128], XF[:, kc, 0:tw], self.ident)
                self.copy("act" if half == 0 else "dve", y[0:tw, half * 512:(half + 1) * 512], b[0:tw, :])
            dst = self.y_p[self.p0 + t0: self.p0 + t0 + tw, :] if t0 < TPP else self.y_s[:, :]
            self.dma("sp", dst, y[0:tw, :], ("yout", ci % 2), reads=[y[0:tw, :]])
        self.release(m)


def nc_rem(nc):
    r = nc.sbuf_bytes_remaining
    return r() if callable(r) else r


def _host_consts():
    c = np.zeros((128, CC_N), np.float32)
    c[:, CC_ID:CC_ID + 128] = np.eye(128, dtype=np.float32)
    c[:, CC_J:CC_J + 128] = np.arange(128, dtype=np.float32)[None, :]
    for h in range(4):
        c[h, CC_OH + 128 * h: CC_OH + 128 * (h + 1)] = 1.0
    sidx = np.arange(128)[:, None]; tidx = np.arange(128)[None, :]
    c[:, CC_BM:CC_BM + 128] = np.where(sidx <= tidx, LN16, 80.0)
    s64 = np.arange(64)[:, None]; t64 = np.arange(64)[None, :]
    c[0:64, CC_BMS:CC_BMS + 64] = np.where((s64 // 4 == t64 // 4) & (s64 <= t64), LN16, 80.0)
    c[0:64, CC_BLK:CC_BLK + 16] = (s64 // 4 == np.arange(16)[None, :]).astype(np.float32)
    m01 = np.ones(TPP + NSAMP, np.float32)
    m01[0:TPP:128] = 0.0; m01[TPP::4] = 0.0
    c[:, CC_M01:CC_M01 + TPP + NSAMP] = m01[None, :]
    m4 = np.ones(64, np.float32); m4[0::4] = 0.0
    c[:, CC_M4:CC_M4 + 64] = m4[None, :]
    for qi in range(4):
        for h in range(4):
            c[h, CC_SEL + 16 * qi + 4 * qi + h] = 1.0
    return c


def _gn(a):
    a = np.asarray(a, np.float32)
    rest = a.shape[2:]
    a = a.reshape((16, 2, 64) + rest)
    a = np.moveaxis(a, 0, 2)
    return np.ascontiguousarray(a.reshape((128, 16) + rest))


def _s5_layouts(inp, l):
    sp = np.zeros((128, 3, 16), np.float32)
    sp[:, 0] = _gn(inp["s5_lambda_re"][l]); sp[:, 1] = _gn(inp["s5_lambda_im"][l])
    sp[:, 2] = _gn(np.repeat(np.asarray(inp["s5_log_dt"][l])[:, None], 64, axis=1))
    Bbd = np.zeros((2, 128, 16, 128), np.float32); Cbd = np.zeros((2, 128, 16, 128), np.float32)
    for part, (bn, cn) in enumerate([("s5_b_re", "s5_c_re"), ("s5_b_im", "s5_c_im")]):
        Bg = _gn(inp[bn][l])
        Cg = _gn(np.transpose(np.asarray(inp[cn][l]), (0, 2, 1)))
        for gc in range(16):
            for gl in range(2):
                c0 = 32 * (gc % 4) + 16 * gl
                Bbd[part, 64 * gl:64 * gl + 64, gc, c0:c0 + 16] = Bg[64 * gl:64 * gl + 64, gc, :]
                Cbd[part, 64 * gl:64 * gl + 64, gc, c0:c0 + 16] = Cg[64 * gl:64 * gl + 64, gc, :]
    return sp, Bbd, Cbd


def _core_inputs(inp, core, cfg=None):
    f = lambda a: np.ascontiguousarray(a, dtype=np.float32)
    m = {}
    m["x_p"] = f(inp["x_prompt"][core])
    m["x_s"] = f(inp["x_sample"][16 * core:16 * core + 16].reshape(NSAMP, D))
    m["p_p"] = f(inp["p_prompt"][:, core])
    m["p_s"] = f(inp["p_sample"][:, 16 * core:16 * core + 16].reshape(NL, NSAMP, DPLE))
    for n in ["w1_gate", "w1_up", "w1_down", "w_in", "s5_w_glu", "w_s5_up", "w_m_up", "w_out", "w2_gate", "w2_up",
              "w2_down", "w_ple", "w_ple_gate"]:
        m[n] = f(inp[n])
    g = np.zeros((128, 4 * NL + 1, 8), np.float32)
    for l in range(NL):
        for k, n in enumerate(["g_ffn1", "g_mix", "g_ffn2", "g_ple"]):
            g[:, 4 * l + k, :] = np.asarray(inp[n][l]).reshape(8, 128).T
    g[:, 4 * NL, :] = np.asarray(inp["g_final"]).reshape(8, 128).T
    m["gcols"] = g
    m["consts"] = _host_consts()
    s5p = np.zeros((NL, 128, 3, 16), np.float32); s5B = np.zeros((NL, 2, 128, 16, 128), np.float32); s5C = np.zeros_like(s5B)
    mixc = np.zeros((128, NL, MIXC_N), np.float32)
    s5h0 = np.zeros((NL, 2, 128, 16, 16), np.float32)
    sl = slice(16 * core, 16 * core + 16)
    for l in range(NL):
        s5p[l], s5B[l], s5C[l] = _s5_layouts(inp, l)
        mixc[:, l, MC_D:MC_D + 4] = np.asarray(inp["s5_d"][l]).reshape(4, 128).T
        mixc[:, l, MC_CW:MC_CW + 64] = np.asarray(inp["conv_w"][l]).reshape(4, 16, 128).transpose(2, 1, 0).reshape(128, 64)
        mixc[:, l, MC_CB:MC_CB + 16] = np.asarray(inp["conv_b"][l]).reshape(16, 128).T
        mixc[:, l, MC_GMH:MC_GMH + 8] = np.asarray(inp["g_mhead"][l]).reshape(8, 128).T
        mixc[0:4, l, MC_BI] = np.asarray(inp["b_igate"][l]); mixc[0:4, l, MC_BF] = np.asarray(inp["b_fgate"][l])
        for part, n in enumerate(["state_s5_re", "state_s5_im"]):
            st = np.asarray(inp[n][l, sl])
            s5h0[l, part] = _gn(np.moveaxis(st, 0, 2))
    m["s5p"] = s5p; m["s5B"] = s5B; m["s5C"] = s5C; m["mixc"] = mixc; m["s5h0"] = s5h0
    m["c0s"] = f(inp["state_mlstm_c"][:, sl])
    n0 = np.asarray(inp["state_mlstm_n"][:, sl], np.float32)
    m["n0T"] = np.ascontiguousarray(n0.reshape(NL, 16, 4, 2, 128).transpose(4, 0, 2, 3, 1))
    m0 = np.zeros((128, NL, 16), np.float32)
    m0[0:4] = np.asarray(inp["state_mlstm_m"][:, sl], np.float32).transpose(2, 0, 1)
    m["m0T"] = m0
    cv = np.asarray(inp["state_conv"][:, sl], np.float32)
    m["conv0T"] = np.ascontiguousarray(cv.reshape(NL, 16, 3, 16, 128).transpose(0, 4, 3, 1, 2))
    return m


def run(inputs, cfg=None, cores=None, trace=False):
    cores = list(range(NCORES)) if cores is None else cores
    prog = Prog(cfg)
    nc = prog.build()
    in_maps = []
    for c in cores:
        cm = _core_inputs(inputs, c, cfg)
        in_maps.append({k: cm[k] for k in prog.ins})
    res = run_bass_kernel_spmd(nc, in_maps, core_ids=list(range(len(cores))), trace=trace)
    return prog, res


def _ungn(a):
    rest = a.shape[2:]
    a = a.reshape((2, 64, 16) + rest)
    a = np.moveaxis(a, 2, 0)
    return a.reshape((32, 64) + rest)


def kernel(**inputs):
    prog, res = run(inputs)
    L = NL
    B = NCORES; SB = 16 * NCORES
    y_p = np.zeros((B, SEQ, D), np.float32); y_s = np.zeros((SB, 4, D), np.float32)
    s5p = [np.zeros((L, B, 32, 64), np.float32) for _ in range(2)]
    s5s = [np.zeros((L, SB, 32, 64), np.float32) for _ in range(2)]
    c_p = np.zeros((L, B, 4, 256, 256), np.float32); n_p = np.zeros((L, B, 4, 256), np.float32)
    m_p = np.zeros((L, B, 4), np.float32); cv_p = np.zeros((L, B, 3, 2048), np.float32)
    c_s = np.zeros((L, SB, 4, 256, 256), np.float32); n_s = np.zeros((L, SB, 4, 256), np.float32)
    m_s = np.zeros((L, SB, 4), np.float32); cv_s = np.zeros((L, SB, 3, 2048), np.float32)
    for c in range(NCORES):
        r = res.results[c]
        sl = slice(16 * c, 16 * c + 16)
        y_p[c] = r["y_p"]; y_s[sl] = np.asarray(r["y_s"]).reshape(16, 4, D)
        for l in range(L):
            for part in range(2):
                s5p[part][l, c] = _ungn(np.asarray(r["o_s5p"][l][:, part]))
                s5s[part][l, sl] = np.moveaxis(_ungn(np.asarray(r["o_s5s"][l][:, part])), 2, 0)
            c_p[l, c] = r["o_cp"][l]
            n_p[l, c] = np.asarray(r["o_np"][l]).transpose(1, 2, 0).reshape(4, 256)
            m_p[l, c] = np.asarray(r["o_mp"][l])[:, 0]
            cv_p[l, c] = np.asarray(r["o_convp"][l]).transpose(2, 1, 0).reshape(3, 2048)
            c_s[l, sl] = r["o_cs"][l]
            n_s[l, sl] = np.asarray(r["o_ns"][l]).transpose(3, 1, 2, 0).reshape(16, 4, 256)
            m_s[l, sl] = np.asarray(r["o_ms"][l]).T
            cv_s[l, sl] = np.asarray(r["o_convs"][l]).transpose(2, 3, 1, 0).reshape(16, 3, 2048)
    return (y_p, y_s, s5p[0], s5p[1], c_p, n_p, m_p, cv_p, s5s[0], s5s[1], c_s, n_s, m_s, cv_s)
```

```python
import math
import numpy as np
from contextlib import ExitStack
import concourse.bass as bass
import concourse.mybir as mybir
from concourse.bass_utils import run_bass_kernel_spmd

F32 = mybir.dt.float32
BF16 = mybir.dt.bfloat16
I32 = mybir.dt.int32
AF = mybir.ActivationFunctionType
ALU = mybir.AluOpType

D = 1024; DFF = 2048; DPLE = 256; NL = 4; G = 32; NS5 = 64; H = 4; DH = 256
IN_COLS = 6664
C_U = 0; C_QK = 512; C_V = 2560; C_O = 3584; C_IG = 4608; C_FG = 4612; C_GS5 = 4616; C_GM = 5640
EPS = 1e-6
LN16 = math.log(16.0)
TWO_PI = 2.0 * math.pi
NCORES = 8
SEQ = 2048
TPP = 1024
NSAMP = 64
_ISZ = {F32: 4, BF16: 2, I32: 4}
MC_D = 0; MC_CW = 4; MC_CB = 68; MC_GMH = 84; MC_BI = 92; MC_BF = 93; MIXC_N = 96
CC_ID = 0; CC_J = 128; CC_OH = 256; CC_BM = 768; CC_BMS = 896; CC_BLK = 960; CC_M01 = 976; CC_M4 = 2064; CC_SEL = 2128; CC_N = 2192


class _Op:
    __slots__ = ("eng", "fn", "deps", "is_dma", "sem_key", "sem_val", "has_dep", "idx", "dma_sem", "prev_total")

    def __init__(self, eng, fn, is_dma, sem_key):
        self.eng = eng; self.fn = fn; self.deps = set(); self.is_dma = is_dma
        self.sem_key = sem_key; self.sem_val = None; self.has_dep = False; self.dma_sem = None; self.prev_total = 0


def _regions(ap):
    isz = _ISZ[ap.dtype]
    dims = ap.ap
    ps, pc = dims[0]
    off = ap.offset
    if ps > 0:
        p_lo = off // ps
        col = off - p_lo * ps
    else:
        p_lo = 0; col = off
    free = [(s, n) for (s, n) in dims[1:] if n > 1 and s != 0]
    if not free:
        return ap.tensor.name, p_lo, p_lo + pc, [(col * isz, (col + 1) * isz)]
    free.sort(key=lambda d: abs(d[0]))
    s0, n0 = free[0]
    if s0 == 1:
        run = n0; outer = free[1:]
    else:
        run = 1; outer = free
    nout = 1
    for s, n in outer:
        nout *= n
    if nout > 48:
        lo = col; hi = col + run
        for s, n in outer:
            hi += s * (n - 1)
        return ap.tensor.name, p_lo, p_lo + pc, [(lo * isz, hi * isz)]
    starts = [col]
    for s, n in outer:
        starts = [b + s * i for b in starts for i in range(n)]
    starts.sort()
    rngs = []
    for b in starts:
        lo = b * isz; hi = (b + run) * isz
        if rngs and rngs[-1][1] >= lo:
            rngs[-1] = (rngs[-1][0], max(hi, rngs[-1][1]))
        else:
            rngs.append((lo, hi))
    return ap.tensor.name, p_lo, p_lo + pc, rngs


class Sched:
    ENGS = ("pe", "act", "dve", "pool", "sp")
    HAND = {"pe": "tensor", "act": "scalar", "dve": "vector", "pool": "gpsimd", "sp": "sync"}

    def __init__(self, nc, n_dma_sems=80):
        self.nc = nc
        self.ops = []
        self.recs = {}
        self.keyw = {}; self.keyr = {}
        self.n_dma_sems = n_dma_sems
        self.dry = False
        self._rc = {}

    def _reg(self, ap):
        k = (ap.tensor.name, ap.offset, ap.ap, ap.dtype)
        r = self._rc.get(k)
        if r is None:
            r = _regions(ap)
            self._rc[k] = r
        return r

    def op(self, eng, fn, reads=(), writes=(), dma=False, sem_key=None):
        if self.dry:
            return None
        o = _Op(eng, fn, dma, sem_key)
        o.idx = len(self.ops)
        deps = o.deps
        rr = []; ww = []
        for a in reads:
            if isinstance(a, (str, tuple)):
                w = self.keyw.get(a)
                if w is not None: deps.add(w)
                self.keyr.setdefault(a, []).append(o.idx)
            else:
                rr.append(self._reg(a))
        for a in writes:
            if isinstance(a, (str, tuple)):
                w = self.keyw.get(a)
                if w is not None: deps.add(w)
                for r in self.keyr.get(a, ()): deps.add(r)
                self.keyw[a] = o.idx; self.keyr[a] = []
            else:
                ww.append(self._reg(a))
        for (name, p0, p1, rngs) in rr:
            lst = self.recs.setdefault(name, [])
            for (lo, hi) in rngs:
                exact = None
                for rec in lst:
                    if rec[2] < hi and lo < rec[3] and rec[0] < p1 and p0 < rec[1]:
                        if rec[4] is not None: deps.add(rec[4])
                        if rec[0] == p0 and rec[1] == p1 and rec[2] == lo and rec[3] == hi:
                            exact = rec
                if exact is not None:
                    exact[5].append(o.idx)
                else:
                    lst.append([p0, p1, lo, hi, None, [o.idx]])
        for (name, p0, p1, rngs) in ww:
            lst = self.recs.setdefault(name, [])
            for (lo, hi) in rngs:
                keep = []
                for rec in lst:
                    if rec[2] < hi and lo < rec[3] and rec[0] < p1 and p0 < rec[1]:
                        if rec[4] is not None: deps.add(rec[4])
                        deps.update(rec[5])
                        if rec[0] >= p0 and rec[1] <= p1 and rec[2] >= lo and rec[3] <= hi:
                            continue
                    keep.append(rec)
                keep.append([p0, p1, lo, hi, o.idx, []])
                lst[:] = keep
        deps.discard(o.idx)
        self.ops.append(o)
        return o

    def emit(self, es):
        nc = self.nc
        ops = self.ops
        for o in ops:
            nd = set()
            for d in o.deps:
                do = ops[d]
                if (not do.is_dma) and (not o.is_dma) and do.eng == "pe" and o.eng == "pe":
                    continue
                nd.add(d)
            o.deps = nd
            for d in nd:
                ops[d].has_dep = True
        dma_keys = {}; dma_cnt = {}
        eng_sem = {e: es.enter_context(nc.semaphore("s_" + e)) for e in self.ENGS}
        dma_sems = []
        eng_cnt = {e: 0 for e in self.ENGS}
        for o in ops:
            if o.is_dma:
                k = o.sem_key
                if k not in dma_keys:
                    if len(dma_sems) < self.n_dma_sems:
                        dma_sems.append(es.enter_context(nc.semaphore("d%d" % len(dma_sems))))
                        dma_keys[k] = len(dma_sems) - 1
                    else:
                        dma_keys[k] = len(dma_keys) % self.n_dma_sems
                si = dma_keys[k]
                o.dma_sem = si
                dma_cnt[si] = dma_cnt.get(si, 0) + 16
                o.sem_val = dma_cnt[si]
                o.prev_total = o.sem_val - 16
                o.has_dep = True
            elif o.has_dep:
                eng_cnt[o.eng] += 1
                o.sem_val = eng_cnt[o.eng]
        per_eng = {e: [o for o in ops if o.eng == e] for e in self.ENGS}
        stats = {"waits": 0}

        def run_engine(e, eng):
            known = {}

            def wait(sid, sem, val):
                if val <= 0 or known.get(sid, 0) >= val:
                    return
                eng.wait_ge(sem, val)
                stats["waits"] += 1
                known[sid] = val

            for o in per_eng[e]:
                for d in sorted(o.deps):
                    do = ops[d]
                    if do.is_dma:
                        wait(("d", do.dma_sem), dma_sems[do.dma_sem], do.sem_val)
                    else:
                        wait(("e", do.eng), eng_sem[do.eng], do.sem_val)
                if o.is_dma:
                    wait(("d", o.dma_sem), dma_sems[o.dma_sem], o.prev_total)
                inst = o.fn(eng)
                if o.is_dma:
                    inst.then_inc(dma_sems[o.dma_sem], 16)
                elif o.has_dep:
                    inst.then_inc(eng_sem[e], 1)
            if e == "sp":
                for si, tot in dma_cnt.items():
                    wait(("d", si), dma_sems[si], tot)

        with nc.Block() as block:
            for e in self.ENGS:
                getattr(block, self.HAND[e])(lambda eng, e=e: run_engine(e, eng))
        self.stats = stats


class Prog:
    def __init__(self, cfg=None):
        cfg = dict(cfg or {})
        self.cfg = cfg
        self.nl = cfg.get("n_layers", NL)
        self.passes = cfg.get("passes", ("A", "B"))
        self.stop = cfg.get("stop", None)
        self.use_s5 = cfg.get("use_s5", True)
        self.use_ml = cfg.get("use_ml", True)
        self.nc = bass.Bass("TRN2", target_bir_lowering=False)
        self.es = ExitStack()
        self.S = Sched(self.nc)
        self.dram = {}
        self.ins = []
        self.outs = []

    def din(self, name, shape, dt=F32):
        t = self.nc.dram_tensor(name, list(shape), dt, kind="ExternalInput").ap()
        self.dram[name] = t; self.ins.append(name)
        return t

    def dout(self, name, shape, dt=F32):
        t = self.nc.dram_tensor(name, list(shape), dt, kind="ExternalOutput").ap()
        self.dram[name] = t; self.outs.append(name)
        return t

    def dump(self, name, ap, dt=F32):
        if not self.cfg.get("dump") or self.S.dry or name in self.dram:
            return
        d = self.dout("dbg_" + name, list(ap.shape), dt)
        self.dma("sp", d, ap, ("dbg", name), reads=[ap])

    def sb(self, name, shape, dt=F32):
        return self.es.enter_context(self.nc.sbuf_tensor(name, list(shape), dt))

    def arena_init(self, nbytes):
        self.arena = self.sb("arena", [128, nbytes // 2], BF16)
        self.arena_bytes = nbytes
        self.atop = 0
        self.amax = 0

    def alloc(self, shape_free, dt):
        n = 1
        for s in shape_free: n *= s
        nb = n * _ISZ[dt]
        nb = (nb + 63) // 64 * 64
        off = self.atop
        self.atop += nb
        self.amax = max(self.amax, self.atop)
        assert self.atop <= self.arena_bytes, ("arena overflow", self.atop, self.arena_bytes)
        return self.view(off, shape_free, dt)

    def view(self, off, shape_free, dt):
        n = 1
        for s in shape_free: n *= s
        v = self.arena[:, off // 2: off // 2 + n * _ISZ[dt] // 2]
        if dt != BF16:
            v = v.bitcast(dt)
        if len(shape_free) == 2:
            v = v.rearrange("p (a b) -> p a b", a=shape_free[0])
        elif len(shape_free) == 3:
            v = v.rearrange("p (a b c) -> p a b c", a=shape_free[0], b=shape_free[1])
        return v

    def mark(self):
        return self.atop

    def release(self, m):
        self.atop = m

    def psum_init(self):
        self.banks = [self.es.enter_context(self.nc.psum_tensor("bank%d" % i, [128, 512], F32)) for i in range(8)]
        self.bi = 0
        self.scr = self.banks[7][0:1, 0:1]

    def bank(self):
        b = self.banks[self.bi % 7]
        self.bi += 1
        return b

    def op(self, eng, fn, reads=(), writes=(), **kw):
        return self.S.op(eng, fn, reads=reads, writes=writes, **kw)

    def act(self, out, in_, func, bias=None, scale=None, accum=None, extra_reads=()):
        kw = {}
        rd = [in_] + list(extra_reads)
        if bias is not None:
            kw["bias"] = bias
            if not isinstance(bias, float): rd.append(bias)
        if scale is not None:
            kw["scale"] = scale
            if not isinstance(scale, float): rd.append(scale)
        wr = [out]
        if accum is not None:
            kw["accum_out"] = accum; wr.append(accum)
        self.op("act", lambda e: e.activation(out=out, in_=in_, func=func, **kw), reads=rd, writes=wr)

    def tt(self, eng, out, a, b, op):
        self.op(eng, lambda e: e.tensor_tensor(out=out, in0=a, in1=b, op=op), reads=[a, b], writes=[out])

    def ts(self, eng, out, a, s1, op0, s2=None, op1=None):
        rd = [a]
        if not isinstance(s1, (float, int)): rd.append(s1)
        if s2 is not None and not isinstance(s2, (float, int)): rd.append(s2)
        if op1 is None:
            self.op(eng, lambda e: e.tensor_scalar(out=out, in0=a, scalar1=s1, scalar2=None, op0=op0), reads=rd, writes=[out])
        else:
            self.op(eng, lambda e: e.tensor_scalar(out=out, in0=a, scalar1=s1, scalar2=s2, op0=op0, op1=op1), reads=rd, writes=[out])

    def stt(self, out, a, s, b, op0, op1):
        rd = [a, b]
        if not isinstance(s, (float, int)): rd.append(s)
        self.op("dve", lambda e: e.scalar_tensor_tensor(out=out, in0=a, scalar=s, in1=b, op0=op0, op1=op1), reads=rd, writes=[out])

    def copy(self, eng, out, in_):
        if eng == "act":
            self.act(out, in_, AF.Copy)
        else:
            self.op(eng, lambda e: e.tensor_copy(out=out, in_=in_), reads=[in_], writes=[out])

    def memset(self, eng, out, val):
        self.op(eng, lambda e: e.memset(out, val), writes=[out])

    def recip(self, out, in_):
        self.op("dve", lambda e: e.reciprocal(out=out, in_=in_), reads=[in_], writes=[out])

    def scan(self, out, d0, d1, init, op0, op1):
        rd = [d0, d1]
        if not isinstance(init, (float, int)): rd.append(init)
        self.op("dve", lambda e: e.tensor_tensor_scan(out=out, data0=d0, data1=d1, initial=init, op0=op0, op1=op1), reads=rd, writes=[out])

    def mm(self, out, pairs, extra_reads=()):
        rd = []
        for l, r in pairs:
            rd.append(l); rd.append(r)
        n = len(pairs)
        is32 = pairs[0][0].dtype == F32
        scr = self.scr; idb = self.identb

        def fn(e):
            inst = None
            for i, (l, r) in enumerate(pairs):
                inst = e.matmul(out, lhsT=l, rhs=r, start=(i == 0), stop=(i == n - 1))
            if is32:
                inst = e.matmul(scr, lhsT=idb[:, 0:1], rhs=idb[:, 0:1], start=True, stop=True)
            return inst
        self.op("pe", fn, reads=rd + list(extra_reads), writes=[out])

    def transpose(self, out, in_, ident):
        self.op("pe", lambda e: e.transpose(out, in_, ident), reads=[in_, ident], writes=[out])

    def transposes(self, triples):
        def fn(e):
            inst = None
            for (o, i, d) in triples:
                inst = e.transpose(o, i, d)
            return inst
        self.op("pe", fn, reads=[t[1] for t in triples] + [t[2] for t in triples], writes=[t[0] for t in triples])

    def dma(self, eng, out, in_, key, reads=(), writes=()):
        self.op(eng, lambda e: e.dma_start(out=out, in_=in_), reads=reads, writes=writes, dma=True, sem_key=key)

    def ws_init(self, nslots):
        self.ws_slots = [self.alloc([2048], BF16) for _ in range(nslots)]
        self.ws_n = nslots
        self.ws_plan = []
        self.ws_i = 0
        self.ws_issued = 0
        self.ws_live = 1

    def ws_reset(self):
        self.ws_i = 0
        self.ws_issued = 0

    def _ws_view(self, k, KC, ncols):
        s = self.ws_slots[k % self.ws_n]
        return s[:, 0:KC * ncols].rearrange("p (kc n) -> p kc n", kc=KC)

    def _ws_issue(self, k):
        W, c0, ncols, KC = self.ws_plan[k]
        dst = self._ws_view(k, KC, ncols)
        src = W.rearrange("(kc p) n -> p kc n", p=128)[:, :, c0:c0 + ncols]
        self.dma("pool", dst, src, ("w", k % self.ws_n), writes=[dst])

    def wget(self, W, c0, ncols, KC, oldest=None):
        if self.S.dry:
            self.ws_plan.append((W, c0, ncols, KC))
            k = len(self.ws_plan) - 1
            self.ws_last = k
            return self._ws_view(k, KC, ncols)
        k = self.ws_i
        self.ws_last = k
        assert self.ws_plan[k][1:] == (c0, ncols, KC), (self.ws_plan[k][1:], (c0, ncols, KC))
        old = k if oldest is None else min(oldest, k)
        assert k - old < self.ws_n
        lim = min(len(self.ws_plan), old + self.ws_n)
        while self.ws_issued < lim:
            self._ws_issue(self.ws_issued)
            self.ws_issued += 1
        self.ws_i += 1
        return self._ws_view(k, KC, ncols)

    def dense_multi(self, streams, ncols, consume, tiles=None):
        tiles = tiles or self.tiles
        nj = ncols // 128
        self.ws_live = len(streams)
        blk = [min(2048 // KC, 512) for (_, _, KC, _) in streams]
        blk = [min(blk)] * len(streams)
        cur = [None] * len(streams)
        curk = [None] * len(streams)
        for j in range(nj):
            for si, (W, c0, KC, rhs_fn) in enumerate(streams):
                nw = blk[si]
                if (j * 128) % nw == 0:
                    others = [curk[s2] for s2 in range(len(streams)) if s2 != si and curk[s2] is not None]
                    cur[si] = self.wget(W, c0 + j * 128, min(nw, ncols - j * 128), KC, oldest=min(others) if others else None)
                    curk[si] = self.ws_last
            for ti, (t0, tw) in enumerate(tiles):
                pss = []
                for si, (W, c0, KC, rhs_fn) in enumerate(streams):
                    nw = blk[si]
                    jo = (j * 128) % nw
                    ps = self.bank()[:, 0:tw]
                    self.mm(ps, [(cur[si][:, kc, jo:jo + 128], rhs_fn(kc, t0, tw)) for kc in range(KC)])
                    pss.append(ps)
                consume(j, ti, t0, tw, pss)

    def build(self):
        nc = self.nc
        with self.es:
            self._declare()
            self.S.dry = True
            self._body()
            self.S.dry = False
            self.ws_reset()
            self.atop = self.a_persist
            self.bi = 0
            self._body()
            self.S.emit(self.es)
        return nc

    def _declare(self):
        L = NL
        din = self.din
        self.x_p = din("x_p", [SEQ, D]); self.x_s = din("x_s", [NSAMP, D])
        self.p_p = din("p_p", [L, SEQ, DPLE]); self.p_s = din("p_s", [L, NSAMP, DPLE])
        for n, k, m in [("w1_gate", D, DFF), ("w1_up", D, DFF), ("w1_down", DFF, D), ("w_in", D, IN_COLS),
                        ("s5_w_glu", 512, 512), ("w_s5_up", 512, D), ("w_m_up", D, D), ("w_out", D, D),
                        ("w2_gate", D, DFF), ("w2_up", D, DFF), ("w2_down", DFF, D), ("w_ple", DPLE, D),
                        ("w_ple_gate", D, D)]:
            setattr(self, n, din(n, [L, k, m]))
        self.gcols_d = din("gcols", [128, 4 * L + 1, 8])
        self.consts_d = din("consts", [128, CC_N])
        self.y_p = self.dout("y_p", [SEQ, D]); self.y_s = self.dout("y_s", [NSAMP, D])
        self.s5p_d = din("s5p", [L, 128, 3, 16])
        self.s5B_d = din("s5B", [L, 2, 128, 16, 128]); self.s5C_d = din("s5C", [L, 2, 128, 16, 128])
        self.s5h0_d = din("s5h0", [L, 2, 128, 16, 16])
        self.mixc_d = din("mixc", [128, L, MIXC_N])
        self.o_s5p = self.dout("o_s5p", [L, 128, 2, 16]); self.o_s5s = self.dout("o_s5s", [L, 128, 2, 16, 16])
        self.c0s_d = din("c0s", [L, 16, 4, 256, 256])
        self.n0T_d = din("n0T", [128, L, 4, 2, 16]); self.m0T_d = din("m0T", [128, L, 16])
        self.conv0T_d = din("conv0T", [L, 128, 16, 16, 3])
        self.o_cp = self.dout("o_cp", [L, 4, 256, 256]); self.o_np = self.dout("o_np", [L, 128, 4, 2])
        self.o_mp = self.dout("o_mp", [L, 4, 1]); self.o_convp = self.dout("o_convp", [L, 128, 16, 3])
        self.o_cs = self.dout("o_cs", [L, 16, 4, 256, 256]); self.o_ns = self.dout("o_ns", [L, 128, 4, 2, 16])
        self.o_ms = self.dout("o_ms", [L, 4, 16]); self.o_convs = self.dout("o_convs", [L, 128, 16, 16, 3])

        self.T = TPP + NSAMP
        T = self.T
        self.xT = self.sb("xT", [128, 8, T], F32)
        self.gcols = self.sb("gcols_sb", [128, 4 * L + 1, 8], F32)
        self.cst = self.sb("cst", [128, CC_N], F32)
        self.ident = self.cst[:, 0:128]
        self.identb = self.sb("identb", [128, 128], BF16)[:]
        self.onesb = self.sb("onesb", [128, 128], BF16)[:]
        self.mixc = self.sb("mixc_sb", [128, L, MIXC_N], F32)
        self.s5car = self.sb("s5car", [128, L, 2, 16], F32)
        self.CST = self.sb("CST", [128, L * 4, 2, 257], F32)
        self.mcar = self.sb("mcar", [128, L], F32)
        self.convt = self.sb("convt", [128, L, 16, 3], F32)
        self.psum_init()
        rem = int(nc_rem(self.nc)) - 2048
        self.arena_init(rem // 128 * 128)
        self.ws_init(6)
        self.xn_off = self.mark()
        self.XN = self.alloc([8, T], BF16)
        self.a_persist = self.mark()

    def _body(self):
        self.dma("sp", self.gcols[:], self.gcols_d, "gcols", writes=[self.gcols[:]])
        self.dma("sp", self.cst[:], self.consts_d, "cst", writes=[self.cst[:]])
        self.dma("sp", self.mixc[:], self.mixc_d, "mixc", writes=[self.mixc[:]])
        self.copy("dve", self.identb[:], self.ident)
        self.memset("dve", self.onesb[:], 1.0)
        for ps in self.passes:
            self._pass(ps)

    def _pass(self, ps):
        T = self.T
        if ps == "A":
            self.ntok_p = TPP; self.ns = NSAMP; self.p0 = 0
        else:
            self.ntok_p = TPP; self.ns = 0; self.p0 = TPP
        self.Tc = self.ntok_p + self.ns
        self.tiles = [(0, 512), (512, 512)] + ([(1024, 64)] if self.ns else [])
        self.chunks = [(c * 128, 128) for c in range(self.ntok_p // 128)] + ([(TPP, 64)] if self.ns else [])
        self.cur_pass = ps
        self._load_x()
        for l in range(self.nl):
            self._layer(l)
        self._final()

    def _load_x(self):
        m = self.mark()
        stg = [self.alloc([D], F32) for _ in range(2)]
        for ci, (t0, tw) in enumerate(self.chunks):
            s = stg[ci % 2]
            src = self.x_p[self.p0 + t0: self.p0 + t0 + tw, :] if t0 < TPP else self.x_s[:, :]
            self.dma("sp", s[0:tw, :], src, ("xin", ci % 2), writes=[s[0:tw, :]])
            for half in range(2):
                b = self.bank()
                for q in range(4):
                    kc = half * 4 + q
                    self.transpose(b[:, q * 128: q * 128 + tw], s[0:tw, kc * 128:(kc + 1) * 128], self.ident[0:tw, 0:tw])
                src_ps = b[:, :].rearrange("p (q t) -> p q t", q=4)[:, :, 0:tw]
                self.copy("act" if half == 0 else "dve", self.xT[:, half * 4: half * 4 + 4, t0:t0 + tw], src_ps)
        self.release(m)

    def _norm(self, gi, out=None):
        out = self.XN if out is None else out
        m = self.mark()
        sq = self.alloc([8, 512], BF16)
        rs = self.alloc([512], F32)
        for (t0, tw) in self.tiles:
            self.act(sq[:, :, 0:tw], self.xT[:, :, t0:t0 + tw], AF.Square)
            ps = self.bank()[:, 0:tw]
            self.mm(ps, [(self.onesb[:], sq[:, kc, 0:tw]) for kc in range(8)])
            self.act(rs[:, 0:tw], ps, AF.Sqrt, bias=EPS, scale=1.0 / D)
            self.recip(rs[:, 0:tw], rs[:, 0:tw])
            for kc in range(8):
                self.stt(out[:, kc, t0:t0 + tw], self.xT[:, kc, t0:t0 + tw], self.gcols[:, gi, kc:kc + 1], rs[:, 0:tw], ALU.mult, ALU.mult)
        self.release(m)

    def _ffn(self, l, wg, wu, wd, gi):
        self._norm(gi)
        m = self.mark()
        HT = self.alloc([16, self.T], BF16)
        tmp = [self.alloc([512], F32) for _ in range(2)]
        XN = self.XN
        cnt = [0]

        def cons_up(j, ti, t0, tw, pss):
            t = tmp[cnt[0] % 2]; cnt[0] += 1
            self.act(t[:, 0:tw], pss[0], AF.Silu)
            self.tt("dve", HT[:, j, t0:t0 + tw], t[:, 0:tw], pss[1], ALU.mult)
        self.dense_multi([(wg[l], 0, 8, lambda kc, t0, tw: XN[:, kc, t0:t0 + tw]),
                          (wu[l], 0, 8, lambda kc, t0, tw: XN[:, kc, t0:t0 + tw])], DFF, cons_up)

        def cons_dn(j, ti, t0, tw, pss):
            self.stt(self.xT[:, j, t0:t0 + tw], pss[0], 0.5, self.xT[:, j, t0:t0 + tw], ALU.mult, ALU.add)
        self.dump("XN", self.XN, BF16); self.dump("HT", HT, BF16)
        self.dense_multi([(wd[l], 0, 16, lambda kc, t0, tw: HT[:, kc, t0:t0 + tw])], D, cons_dn)
        self.release(m)

    def _layer(self, l):
        self._ffn(l, self.w1_gate, self.w1_up, self.w1_down, 4 * l + 0)
        if self.stop == "ffn1":
            return
        self._mixer(l)
        if self.stop == "mixer":
            return
        self._ffn(l, self.w2_gate, self.w2_up, self.w2_down, 4 * l + 2)
        self._ple(l)

    def _ple(self, l):
        self._norm(4 * l + 3)
        m = self.mark()
        PT2 = self.alloc([2, self.T], BF16)
        stg = [self.alloc([DPLE], F32) for _ in range(2)]
        tmp = [self.alloc([512], F32) for _ in range(2)]
        for ci, (t0, tw) in enumerate(self.chunks):
            sg = stg[ci % 2]
            src = self.p_p[l, self.p0 + t0: self.p0 + t0 + tw, :] if t0 < TPP else self.p_s[l]
            self.dma("sp", sg[0:tw, :], src, ("pin", ci % 2), writes=[sg[0:tw, :]])
            b = self.bank()
            for kc in range(2):
                self.transpose(b[:, kc * 128: kc * 128 + tw], sg[0:tw, kc * 128:(kc + 1) * 128], self.ident[0:tw, 0:tw])
            self.copy("act", PT2[:, :, t0:t0 + tw], b[:, 0:256].rearrange("p (k t) -> p k t", k=2)[:, :, 0:tw])
        XN = self.XN
        cnt = [0]

        def cons(j, ti, t0, tw, pss):
            t = tmp[cnt[0] % 2]; cnt[0] += 1
            self.act(t[:, 0:tw], pss[0], AF.Sigmoid)
            self.tt("dve", t[:, 0:tw], t[:, 0:tw], pss[1], ALU.mult)
            self.tt("dve", self.xT[:, j, t0:t0 + tw], self.xT[:, j, t0:t0 + tw], t[:, 0:tw], ALU.add)
        self.dense_multi([(self.w_ple_gate[l], 0, 8, lambda kc, t0, tw: XN[:, kc, t0:t0 + tw]),
                          (self.w_ple[l], 0, 2, lambda kc, t0, tw: PT2[:, kc, t0:t0 + tw])], D, cons)
        self.release(m)

    def _rr_sin(self, out, x, xs, phase, tu, ti, tg):
        sc = 1.0 / TWO_PI
        self.ts("dve", tu, x, xs * sc, ALU.mult, phase * sc, ALU.add)
        self.copy("dve", ti, tu)
        self.copy("dve", tg, ti)
        self.tt("dve", tu, tu, tg, ALU.subtract)
        self.ts("dve", tg, tu, 0.5, ALU.is_gt)
        self.tt("dve", tu, tu, tg, ALU.subtract)
        self.ts("dve", tg, tu, -0.5, ALU.is_lt)
        self.tt("dve", tu, tu, tg, ALU.add)
        self.act(out, tu, AF.Sin, scale=TWO_PI * (1.0 - 1e-6))

    def _mixer(self, l):
        self._norm(4 * l + 1)
        mm0 = self.mark()
        T = self.T
        XN = self.XN
        UY = self.alloc([4, T], BF16)
        self.UY = UY

        def cons_u(j, ti, t0, tw, pss):
            self.copy("act", UY[:, j, t0:t0 + tw], pss[0])
        self.dense_multi([(self.w_in[l], C_U, 8, lambda kc, t0, tw: XN[:, kc, t0:t0 + tw])], 512, cons_u)
        if self.use_s5:
            self._s5(l)
            self._norm(4 * l + 1)
        if self.use_ml:
            self.HMT = self.alloc([8, T], BF16)
            self._mlstm(l)
        m = self.mark()
        MG = self.alloc([8, T], BF16)
        tmp = [self.alloc([512], F32) for _ in range(4)]
        cnt = [0]
        streams = []
        if self.use_s5:
            streams += [(self.w_in[l], C_GS5, 8, lambda kc, t0, tw: XN[:, kc, t0:t0 + tw]),
                        (self.w_s5_up[l], 0, 4, lambda kc, t0, tw: UY[:, kc, t0:t0 + tw])]
        if self.use_ml:
            HMT = self.HMT
            streams += [(self.w_in[l], C_GM, 8, lambda kc, t0, tw: XN[:, kc, t0:t0 + tw]),
                        (self.w_m_up[l], 0, 8, lambda kc, t0, tw: HMT[:, kc, t0:t0 + tw])]

        def cons_m(j, ti, t0, tw, pss):
            k = cnt[0] % 2; cnt[0] += 1
            ta = tmp[2 * k]; tb = tmp[2 * k + 1]
            self.act(ta[:, 0:tw], pss[0], AF.Sigmoid)
            if len(pss) == 2:
                self.tt("dve", MG[:, j, t0:t0 + tw], ta[:, 0:tw], pss[1], ALU.mult)
            else:
                self.tt("dve", ta[:, 0:tw], ta[:, 0:tw], pss[1], ALU.mult)
                self.act(tb[:, 0:tw], pss[2], AF.Sigmoid)
                self.tt("dve", tb[:, 0:tw], tb[:, 0:tw], pss[3], ALU.mult)
                self.tt("pool", MG[:, j, t0:t0 + tw], ta[:, 0:tw], tb[:, 0:tw], ALU.add)
        self.dense_multi(streams, D, cons_m)

        def cons_o(j, ti, t0, tw, pss):
            self.tt("dve", self.xT[:, j, t0:t0 + tw], self.xT[:, j, t0:t0 + tw], pss[0], ALU.add)
        self.dense_multi([(self.w_out[l], 0, 8, lambda kc, t0, tw: MG[:, kc, t0:t0 + tw])], D, cons_o)
        self.release(m)
        self.release(mm0)

    def _s5(self, l):
        m0 = self.mark()
        UY = self.UY
        A = ALU
        isA = self.cur_pass == "A"
        SM = self.alloc([40, 16], F32)
        sm = lambda i: SM[:, i, :]
        LRE, LIM, LDT, DT, R_, TH, CTH, STH, ARE, AIM, DEN, PR, WRE, WIM, T1, T2, C128, S128, TH128, TU, TG, FR, FI = [sm(i) for i in range(23)]
        SMI = self.alloc([16], I32)
        COS = self.alloc([16, 128], F32); SIN = self.alloc([16, 128], F32)
        BBT = [self.alloc([16, 128], BF16) for _ in range(2)]
        CT = [self.alloc([16, 128], BF16) for _ in range(2)]
        W = [self.view(self.xn_off, [16, 128], F32), self.view(self.xn_off + 8192, [16, 128], F32),
             self.alloc([16, 128], F32), self.alloc([16, 128], F32)]
        WI = W[3].bitcast(I32)
        HB = [self.alloc([16, 128], BF16) for _ in range(2)]
        YT = self.alloc([4, 128], F32); YG = self.alloc([4, 128], BF16); SG = self.alloc([4, 128], BF16)
        INIT = self.s5car[:, l]
        sp = self.alloc([3, 16], F32)
        self.dma("sp", sp, self.s5p_d[l], "s5p", writes=[sp])
        self.copy("dve", LRE, sp[:, 0, :]); self.copy("dve", LIM, sp[:, 1, :])
        self.act(DT, sp[:, 2, :], AF.Exp)
        self.tt("dve", T1, LRE, DT, A.mult)
        self.act(R_, T1, AF.Exp)
        self.tt("dve", TH, LIM, DT, A.mult)
        self.act(T2, TH, AF.Abs)
        self._rr_sin(STH, T2, 1.0, 0.0, TU, SMI, TG)
        self._rr_sin(CTH, T2, 1.0, math.pi / 2, TU, SMI, TG)
        self.act(T1, TH, AF.Sign)
        self.tt("dve", STH, STH, T1, A.mult)
        self.tt("dve", ARE, R_, CTH, A.mult); self.tt("dve", AIM, R_, STH, A.mult)
        self.tt("dve", DEN, LRE, LRE, A.mult); self.tt("dve", T1, LIM, LIM, A.mult)
        self.tt("dve", DEN, DEN, T1, A.add); self.recip(DEN, DEN)
        self.ts("dve", PR, ARE, -1.0, A.add)
        self.tt("dve", WRE, PR, LRE, A.mult); self.tt("dve", T1, AIM, LIM, A.mult)
        self.tt("dve", WRE, WRE, T1, A.add); self.tt("dve", WRE, WRE, DEN, A.mult)
        self.tt("dve", WIM, AIM, LRE, A.mult); self.tt("dve", T1, PR, LIM, A.mult)
        self.tt("dve", WIM, WIM, T1, A.subtract); self.tt("dve", WIM, WIM, DEN, A.mult)
        jrow = self.cst[:, CC_J:CC_J + 128]
        ANG = W[0]
        self.tt("dve", ANG, T2.unsqueeze(2).broadcast_to([128, 16, 128]), jrow.unsqueeze(1).broadcast_to([128, 16, 128]), A.mult)
        f2 = lambda a: a.rearrange("p a b -> p (a b)")
        self._rr_sin(f2(SIN), f2(ANG), 1.0, 0.0, f2(W[1]), f2(WI), f2(W[2]))
        self._rr_sin(f2(COS), f2(ANG), 1.0, math.pi / 2, f2(W[1]), f2(WI), f2(W[2]))
        self.act(T1, TH, AF.Sign)
        self.tt("dve", SIN, SIN, T1.unsqueeze(2).broadcast_to([128, 16, 128]), A.mult)
        self.ts("dve", TH128, T2, 128.0, A.mult)
        self._rr_sin(S128, TH128, 1.0, 0.0, TU, SMI, TG)
        self._rr_sin(C128, TH128, 1.0, math.pi / 2, TU, SMI, TG)
        self.tt("dve", S128, S128, T1, A.mult)
        XB = [W[0], W[1]]
        self.dma("sp", XB[0], self.s5B_d[l, 0], "s5x0", writes=[XB[0]])
        self.dma("sp", XB[1], self.s5B_d[l, 1], "s5x1", writes=[XB[1]])
        bc = lambda a: a.unsqueeze(2).broadcast_to([128, 16, 128])
        self.tt("dve", W[2], XB[0], bc(WRE), A.mult); self.tt("pool", W[3], XB[1], bc(WIM), A.mult)
        self.tt("dve", W[2], W[2], W[3], A.subtract)
        self.tt("pool", W[3], XB[1], bc(WRE), A.mult); self.tt("dve", XB[1], XB[0], bc(WIM), A.mult)
        self.tt("dve", W[3], W[3], XB[1], A.add)
        for part in range(2):
            for q in range(4):
                b = self.bank()
                for gi in range(4):
                    self.transpose(b[:, gi * 128:(gi + 1) * 128], W[2 + part][:, 4 * q + gi, :], self.ident)
                self.copy("act" if q % 2 == 0 else "dve", BBT[part][:, 4 * q:4 * q + 4, :], b[:, :].rearrange("p (a b) -> p a b", a=4))
        self.dma("sp", W[0], self.s5C_d[l, 0], "s5x0", writes=[W[0]])
        self.dma("sp", W[1], self.s5C_d[l, 1], "s5x1", writes=[W[1]])
        self.copy("dve", CT[0], W[0])
        self.act(CT[1], W[1], AF.Copy, scale=-1.0)
        WG = self.wget(self.s5_w_glu[l], 0, 512, 4)
        Dbc = self.mixc[:, l, MC_D:MC_D + 4]
        if isA:
            self.memset("dve", INIT, 0.0)

        def y_and_glu(t0, tw, HBv):
            Y = self.bank()
            for fc in range(4):
                prs = []
                for gc in range(4 * fc, 4 * fc + 4):
                    prs.append((CT[0][:, gc, :], HBv[0][:, gc, 0:tw])); prs.append((CT[1][:, gc, :], HBv[1][:, gc, 0:tw]))
                self.mm(Y[:, fc * tw:(fc + 1) * tw], prs)
            Y3 = Y[:, 0:4 * tw].rearrange("p (a b) -> p a b", a=4)
            self.tt("pool", YT[:, :, 0:tw], UY[:, :, t0:t0 + tw], Dbc.unsqueeze(2).broadcast_to([128, 4, tw]), A.mult)
            self.tt("dve", YT[:, :, 0:tw], YT[:, :, 0:tw], Y3, A.add)
            self.act(YG[:, :, 0:tw], YT[:, :, 0:tw], AF.Gelu)
            PG = self.bank()
            for oc in range(4):
                self.mm(PG[:, oc * tw:(oc + 1) * tw], [(WG[:, kc, oc * 128:(oc + 1) * 128], YG[:, kc, 0:tw]) for kc in range(4)])
            self.act(SG[:, :, 0:tw], PG[:, 0:4 * tw].rearrange("p (a b) -> p a b", a=4), AF.Sigmoid)
            self.tt("dve", UY[:, :, t0:t0 + tw], YG[:, :, 0:tw], SG[:, :, 0:tw], A.mult)

        nch = self.ntok_p // 128
        for ci in range(nch):
            t0 = ci * 128
            for q in range(4):
                bre = self.bank(); bim = self.bank()
                for gi in range(4):
                    gc = 4 * q + gi
                    self.mm(bre[:, gi * 128:(gi + 1) * 128], [(BBT[0][:, gc, :], UY[:, q, t0:t0 + 128])])
                    self.mm(bim[:, gi * 128:(gi + 1) * 128], [(BBT[1][:, gc, :], UY[:, q, t0:t0 + 128])])
                r3 = bre[:, :].rearrange("p (a b) -> p a b", a=4); i3 = bim[:, :].rearrange("p (a b) -> p a b", a=4)
                qs = slice(4 * q, 4 * q + 4)
                ae = "dve" if q == 3 else "pool"
                self.tt("dve", W[0][:, qs, :], r3, COS[:, qs, :], A.mult)
                self.tt("dve", W[2][:, qs, :], i3, SIN[:, qs, :], A.mult)
                self.tt(ae, W[0][:, qs, :], W[0][:, qs, :], W[2][:, qs, :], A.add)
                self.tt("dve", W[1][:, qs, :], i3, COS[:, qs, :], A.mult)
                self.tt("dve", W[3][:, qs, :], r3, SIN[:, qs, :], A.mult)
                self.tt(ae, W[1][:, qs, :], W[1][:, qs, :], W[3][:, qs, :], A.subtract)
            for gc in range(16):
                for part in range(2):
                    self.scan(W[2 + part][:, gc, :], R_[:, gc:gc + 1].broadcast_to([128, 128]), W[part][:, gc, :],
                              INIT[:, part, gc:gc + 1], A.mult, A.add)
            self.copy("dve", FR, W[2][:, :, 127]); self.copy("dve", FI, W[3][:, :, 127])
            self.tt("dve", T1, FR, C128, A.mult); self.tt("dve", TU, FI, S128, A.mult)
            self.tt("dve", INIT[:, 0, :], T1, TU, A.subtract)
            self.tt("dve", T1, FR, S128, A.mult); self.tt("dve", TU, FI, C128, A.mult)
            self.tt("dve", INIT[:, 1, :], T1, TU, A.add)
            lo = slice(0, 8); hi = slice(8, 16)
            self.tt("pool", W[1][:, hi], W[3][:, hi], SIN[:, hi], A.mult)
            self.tt("dve", W[0], W[2], COS, A.mult); self.tt("dve", W[1][:, lo], W[3][:, lo], SIN[:, lo], A.mult)
            self.tt("dve", HB[0], W[0], W[1], A.subtract)
            self.tt("pool", W[1][:, hi], W[3][:, hi], COS[:, hi], A.mult)
            self.tt("dve", W[0], W[2], SIN, A.mult); self.tt("dve", W[1][:, lo], W[3][:, lo], COS[:, lo], A.mult)
            self.tt("dve", HB[1], W[0], W[1], A.add)
            y_and_glu(t0, 128, HB)
        if not isA:
            OS = self.alloc([2, 16], F32)
            self.tt("dve", T1, FR, COS[:, :, 127], A.mult); self.tt("dve", TU, FI, SIN[:, :, 127], A.mult)
            self.tt("dve", OS[:, 0, :], T1, TU, A.subtract)
            self.tt("dve", T1, FR, SIN[:, :, 127], A.mult); self.tt("dve", TU, FI, COS[:, :, 127], A.mult)
            self.tt("dve", OS[:, 1, :], T1, TU, A.add)
            self.dma("sp", self.o_s5p[l], OS, "os5p", reads=[OS])
        else:
            t0 = TPP
            H0 = self.alloc([2, 16, 16], F32)
            self.dma("sp", H0, self.s5h0_d[l].rearrange("r p a b -> p r a b"), "s5h0", writes=[H0])
            X0 = self.alloc([2, 16, 16], F32)
            TS = self.alloc([16, 16], F32)
            b16 = lambda a: a.unsqueeze(2).broadcast_to([128, 16, 16])
            self.tt("dve", X0[:, 0], H0[:, 0], b16(ARE), A.mult); self.tt("dve", TS, H0[:, 1], b16(AIM), A.mult)
            self.tt("dve", X0[:, 0], X0[:, 0], TS, A.subtract)
            self.tt("dve", X0[:, 1], H0[:, 1], b16(ARE), A.mult); self.tt("dve", TS, H0[:, 0], b16(AIM), A.mult)
            self.tt("dve", X0[:, 1], X0[:, 1], TS, A.add)
            RM = self.alloc([16, 64], F32)
            m4 = self.cst[:, CC_M4:CC_M4 + 64]
            self.tt("dve", RM, R_.unsqueeze(2).broadcast_to([128, 16, 64]), m4.unsqueeze(1).broadcast_to([128, 16, 64]), A.mult)
            w4 = lambda a, qs: a[:, qs, 0:64].rearrange("p a (s j) -> p a s j", j=4)
            for q in range(4):
                bre = self.bank(); bim = self.bank()
                for gi in range(4):
                    gc = 4 * q + gi
                    self.mm(bre[:, gi * 64:(gi + 1) * 64], [(BBT[0][:, gc, :], UY[:, q, t0:t0 + 64])])
                    self.mm(bim[:, gi * 64:(gi + 1) * 64], [(BBT[1][:, gc, :], UY[:, q, t0:t0 + 64])])
                qs = slice(4 * q, 4 * q + 4)
                r4 = bre[:, 0:256].rearrange("p (a s j) -> p a s j", a=4, j=4); i4 = bim[:, 0:256].rearrange("p (a s j) -> p a s j", a=4, j=4)
                c4 = COS[:, qs, 0:4].unsqueeze(2).broadcast_to([128, 4, 16, 4]); s4 = SIN[:, qs, 0:4].unsqueeze(2).broadcast_to([128, 4, 16, 4])
                self.tt("dve", w4(W[0], qs), r4, c4, A.mult)
                self.tt("dve", w4(W[2], qs), i4, s4, A.mult)
                self.tt("pool", w4(W[0], qs), w4(W[0], qs), w4(W[2], qs), A.add)
                self.tt("dve", w4(W[1], qs), i4, c4, A.mult)
                self.tt("dve", w4(W[3], qs), r4, s4, A.mult)
                self.tt("pool", w4(W[1], qs), w4(W[1], qs), w4(W[3], qs), A.subtract)
            for part in range(2):
                v0 = W[part][:, :, 0:64].rearrange("p a (s j) -> p a s j", j=4)[:, :, :, 0]
                self.tt("dve", v0, v0, X0[:, part], A.add)
            for gc in range(16):
                for part in range(2):
                    self.scan(W[2 + part][:, gc, 0:64], RM[:, gc, :], W[part][:, gc, 0:64], 0.0, A.mult, A.add)
            al = slice(0, 16)
            g4r = W[2][:, :, 0:64].rearrange("p a (s j) -> p a s j", j=4); g4i = W[3][:, :, 0:64].rearrange("p a (s j) -> p a s j", j=4)
            ca = COS[:, :, 0:4].unsqueeze(2).broadcast_to([128, 16, 16, 4]); sa = SIN[:, :, 0:4].unsqueeze(2).broadcast_to([128, 16, 16, 4])
            self.tt("dve", w4(W[0], al), g4r, ca, A.mult); self.tt("pool", w4(W[1], al), g4i, sa, A.mult)
            OSS = self.alloc([2, 16, 16], F32)
            j3 = lambda a: a[:, :, 0:64].rearrange("p a (s j) -> p a s j", j=4)[:, :, :, 3]
            self.tt("dve", HB[0][:, :, 0:64], W[0][:, :, 0:64], W[1][:, :, 0:64], A.subtract)
            self.tt("dve", OSS[:, 0], j3(W[0]), j3(W[1]), A.subtract)
            self.tt("dve", w4(W[0], al), g4r, sa, A.mult); self.tt("pool", w4(W[1], al), g4i, ca, A.mult)
            self.tt("dve", HB[1][:, :, 0:64], W[0][:, :, 0:64], W[1][:, :, 0:64], A.add)
            self.tt("dve", OSS[:, 1], j3(W[0]), j3(W[1]), A.add)
            self.dma("sp", self.o_s5s[l], OSS, "os5s", reads=[OSS])
            y_and_glu(t0, 64, HB)
        self.release(m0)

    def _mnorm(self, TOT, e2_col, P, HN, sm, SQ):
        A = ALU
        S257, Tm, Av, Vv, Sc = [sm[0:P, i:i + 1] for i in range(5)]
        self.tt("dve", SQ[0:P, :], TOT[0:P, :], TOT[0:P, :], A.mult)
        self.op("dve", lambda e: e.reduce_sum(out=S257, in_=SQ[0:P, :], axis=mybir.AxisListType.X), reads=[SQ[0:P, :]], writes=[S257])
        self.ts("dve", Tm, SQ[0:P, 256:257], e2_col, A.max)
        self.tt("dve", Av, S257, SQ[0:P, 256:257], A.subtract)
        self.stt(Vv, Tm, EPS * float(DH), Av, A.mult, A.add)
        self.act(Vv, Vv, AF.Ln, scale=1.0 / DH)
        self.act(Sc, Vv, AF.Exp, scale=-0.5)
        self.act(HN[0:P, :], TOT[0:P, 0:256], AF.Copy, scale=Sc)

    def _mlstm(self, l):
        m0 = self.mark()
        A = ALU
        isA = self.cur_pass == "A"
        T = self.T; Tc = self.Tc; NP = self.ntok_p
        XN = self.XN; HMT = self.HMT; cst = self.cst; mixc = self.mixc
        nchp = NP // 128
        rowM = self.alloc([T], F32)
        M = rowM[0:4, :]
        MH = self.alloc([T], BF16)[0:4, :]; ML_ = self.alloc([T], BF16)[0:4, :]
        OHB = self.alloc([512], BF16)
        self.copy("dve", OHB[0:4, :], self.cst[0:4, CC_OH:CC_OH + 512])
        SMALL = self.alloc([128], F32)
        MP = SMALL[0:4, 0:24]; ML = SMALL[0:4, 24:48]; BL = SMALL[0:4, 48:72]; DEC = SMALL[0:4, 72:96]; MNEW = SMALL[0:4, 96:120]
        MS0 = self.alloc([16], F32)
        COLS = self.alloc([9, 16], F32)
        DECB = self.alloc([4, 24], F32)
        oh = lambda hd, P=128: cst[0:4, CC_OH + 128 * hd: CC_OH + 128 * hd + P]
        ohb = lambda hd, P=128: OHB[0:4, 128 * hd: 128 * hd + P]
        mrow = self.mark()
        rows = [self.alloc([T], F32) for _ in range(4)] + [rowM]
        LI, LF, B, D0 = [r[0:4, :] for r in rows[0:4]]
        X1, X2 = [self.alloc([T], F32)[0:4, :] for _ in range(2)]
        self.ws_live = 1
        WGt = self.wget(self.w_in[l], C_IG, 8, 8)
        for (t0, tw) in self.tiles:
            pi = self.bank()[0:4, 0:tw]; pf = self.bank()[0:4, 0:tw]
            self.mm(pi, [(WGt[:, kc, 0:4], XN[:, kc, t0:t0 + tw]) for kc in range(8)])
            self.mm(pf, [(WGt[:, kc, 4:8], XN[:, kc, t0:t0 + tw]) for kc in range(8)])
            self.act(LI[:, t0:t0 + tw], pi, AF.Identity, bias=mixc[0:4, l, MC_BI:MC_BI + 1])
            self.act(D0[:, t0:t0 + tw], pf, AF.Identity, bias=mixc[0:4, l, MC_BF:MC_BF + 1])
        al = slice(0, Tc)
        self.act(X1[:, al], D0[:, al], AF.Abs)
        self.act(X1[:, al], X1[:, al], AF.Exp, scale=-1.0)
        self.act(X1[:, al], X1[:, al], AF.Ln, bias=1.0)
        self.ts("dve", X2[:, al], D0[:, al], 0.0, A.min)
        self.tt("dve", LF[:, al], X2[:, al], X1[:, al], A.subtract)
        self.scan(B[:, al], cst[0:4, CC_M01:CC_M01 + Tc], LF[:, al], 0.0, A.mult, A.add)
        self.tt("dve", LI[:, al], LI[:, al], B[:, al], A.subtract)
        C_ = LI
        p3 = lambda a: a[:, 0:NP].rearrange("p (c j) -> p c j", j=128)
        s4 = lambda a: a[:, NP:NP + NSAMP].rearrange("p (s j) -> p s j", j=4)
        if isA:
            self.memset("dve", self.mcar[0:4, l:l + 1], 0.0)
        self.memset("dve", D0[:, 0:NP], 0.0)
        self.copy("dve", p3(D0)[:, 1:nchp, 0], p3(B)[:, 0:nchp - 1, 127])
        self.scan(M[:, 0:NP], D0[:, 0:NP], C_[:, 0:NP], self.mcar[0:4, l:l + 1], A.add, A.max)
        self.copy("dve", MP[:, 0:1], self.mcar[0:4, l:l + 1])
        self.tt("dve", MP[:, 1:nchp], p3(B)[:, 0:nchp - 1, 127], p3(M)[:, 0:nchp - 1, 127], A.add)
        self.copy("dve", ML[:, 0:nchp], p3(M)[:, :, 127]); self.copy("dve", BL[:, 0:nchp], p3(B)[:, :, 127])
        ncol = nchp
        if isA:
            self.dma("sp", MS0[0:4, :], self.m0T_d[0:4, l, :], "ms0", writes=[MS0[0:4, :]])
            self.tt("dve", s4(M)[:, :, 0], MS0[0:4, :], s4(C_)[:, :, 0], A.max)
            for j in range(1, 4):
                self.tt("dve", s4(M)[:, :, j], s4(M)[:, :, j - 1], s4(C_)[:, :, j], A.max)
            self.copy("dve", MP[:, 8:24], MS0[0:4, :])
            self.copy("dve", ML[:, 8:24], s4(M)[:, :, 3]); self.copy("dve", BL[:, 8:24], s4(B)[:, :, 3])
            ncol = 24
        self.tt("dve", DEC[:, 0:ncol], MP[:, 0:ncol], ML[:, 0:ncol], A.subtract)
        self.act(DEC[:, 0:ncol], DEC[:, 0:ncol], AF.Exp)
        self.tt("dve", MNEW[:, 0:ncol], BL[:, 0:ncol], ML[:, 0:ncol], A.add)
        self.copy("dve", self.mcar[0:4, l:l + 1], MNEW[:, nchp - 1:nchp])
        self.tt("dve", p3(D0), MP[:, 0:nchp].unsqueeze(2).broadcast_to([4, nchp, 128]), p3(M), A.subtract)
        self.tt("dve", p3(X2), p3(C_), ML[:, 0:nchp].unsqueeze(2).broadcast_to([4, nchp, 128]), A.subtract)
        if isA:
            self.tt("dve", s4(D0), MP[:, 8:24].unsqueeze(2).broadcast_to([4, 16, 4]), s4(M), A.subtract)
            self.tt("dve", s4(X2), s4(C_), ML[:, 8:24].unsqueeze(2).broadcast_to([4, 16, 4]), A.subtract)
        self.act(D0[:, al], D0[:, al], AF.Exp, bias=-LN16)
        self.act(LF[:, al], X2[:, al], AF.Exp)
        self.tt("dve", X1[:, al], B[:, al], M[:, al], A.add)
        self.act(B[:, al], X1[:, al], AF.Exp, scale=-2.0)
        A_ = D0; E_r = B; WK = LF
        self.copy("dve", MH[:, al], M[:, al])
        self.copy("dve", X1[:, al], MH[:, al])
        self.tt("dve", X1[:, al], M[:, al], X1[:, al], A.subtract)
        self.copy("dve", ML_[:, al], X1[:, al])
        cb = self.bank()
        if self.cfg.get("pe_probe"):
            dmy = self.bank()
            for _ in range(self.cfg["pe_probe"]):
                self.mm(dmy[0:128, 0:16], [(C_[:, 0:128], cst[0:4, CC_SEL: CC_SEL + 16])])
        for ci, (t0, tw) in enumerate(self.chunks):
            self.mm(cb[0:tw, ci * 16: ci * 16 + 16],
                    [(rt[:, t0:t0 + tw], cst[0:4, CC_SEL + 16 * qi: CC_SEL + 16 * qi + 16]) for qi, rt in enumerate([C_, A_, E_r, WK])])
        nck = len(self.chunks)
        self.copy("dve", COLS[:, 0:nck, :], cb[:, 0:nck * 16].rearrange("p (c q) -> p c q", q=16))
        if self.cfg.get("dump"):
            CBD = self.alloc([144], F32)
            self.copy("act", CBD, cb[:, 0:144])
            self.dump("CBD", CBD)
        self.dump("COLS_a", COLS)
        col = lambda ci, qi, hd, P=128: COLS[0:P, ci, qi * 4 + hd: qi * 4 + hd + 1]
        db = self.bank()
        for hd in range(4):
            self.mm(db[:, hd * 32: hd * 32 + ncol], [(oh(hd), DEC[:, 0:ncol])])
        self.copy("dve", DECB[:, :, 0:ncol], db[:, 0:128].rearrange("p (h c) -> p h c", h=4)[:, :, 0:ncol])
        self.dump("COLS", COLS); self.dump("rowC", rows[0]); self.dump("rowWK", rows[1]); self.dump("rowE", rows[2]); self.dump("rowA", rows[3]); self.dump("rowM", rows[4]); self.dump("DECB", DECB)
        self.dump("COLS_b", COLS)
        self.release(mrow)
        QT = self.alloc([2, T], BF16); KT = self.alloc([2, T], BF16)
        QS32 = self.alloc([2, 64], F32)
        VT = self.alloc([9, 257], BF16)
        RAW = self.alloc([515], F32); RAWS = self.alloc([16, 7], F32); ACC = self.alloc([512], F32)
        CnB = self.alloc([2, 257], BF16)
        ARG = self.alloc([128], F32); Wt = self.alloc([128], F32); PT = self.alloc([128], BF16)
        T2 = self.alloc([257], F32); TOT = self.alloc([257], F32); HN = self.alloc([256], BF16)
        KW2 = [self.alloc([256], BF16) for _ in range(2)]; KWS = self.alloc([256], BF16)
        smn = self.alloc([8], F32)
        N1SB = [self.alloc([257], F32) for _ in range(2)]
        if isA:
            CS = [self.alloc([2, 257], F32) for _ in range(2)]; CO = [self.alloc([2, 257], F32) for _ in range(2)]
            VBD = [self.alloc([258], BF16) for _ in range(2)]
            N2ACC = self.alloc([257], F32); N1S = self.alloc([257], F32)
            CSB = self.alloc([2, 257], BF16)
            N0T = self.alloc([4, 2, 16], F32); NOUT = self.alloc([4, 2, 16], F32)
            self.dma("sp", N0T, self.n0T_d[:, l], "n0t", writes=[N0T])
        else:
            NOUTP = self.alloc([4, 2], F32)
        self.memset("dve", VT[:, :, 256:257], 1.0)
        bigm = cst[:, CC_BM:CC_BM + 128]; bigms = cst[0:64, CC_BMS:CC_BMS + 64]
        cw = lambda ch, j: mixc[:, l, MC_CW + ch * 4 + j: MC_CW + ch * 4 + j + 1]
        cbias = lambda ch: mixc[:, l, MC_CB + ch: MC_CB + ch + 1]
        ntl = len(self.tiles)
        evq = [0]
        for hd in range(4):
            for which, DST in ((0, QT), (1, KT)):
                c0 = C_QK + which * 1024 + hd * 256

                def cons_qk(j, ti, t0, tw, pss, which=which, DST=DST):
                    ch = which * 8 + hd * 2 + j
                    if t0 < NP:
                        if ti == 0:
                            if isA:
                                self.memset("dve", RAW[:, 0:3], 0.0)
                            else:
                                self.copy("dve", RAW[:, 0:3], self.convt[:, l, ch, :])
                        self.copy("act", RAW[:, 3:3 + tw], pss[0])
                        self.ts("dve", ACC[:, 0:tw], RAW[:, 0:tw], cw(ch, 0), A.mult, cbias(ch), A.add)
                        for jj in range(1, 4):
                            self.stt(ACC[:, 0:tw], RAW[:, jj:jj + tw], cw(ch, jj), ACC[:, 0:tw], A.mult, A.add)
                        self.act(DST[:, j, t0:t0 + tw], ACC[:, 0:tw], AF.Silu)
                        if t0 + tw < NP:
                            self.copy("dve", RAW[:, 0:3], RAW[:, tw:tw + 3])
                        else:
                            self.copy("dve", self.convt[:, l, ch, :], RAW[:, tw:tw + 3])
                    else:
                        self.dma("sp", RAWS[:, :, 0:3], self.conv0T_d[l, :, ch], "raws", writes=[RAWS[:, :, 0:3]])
                        self.copy("act", RAWS[:, :, 3:7], pss[0].rearrange("p (s j) -> p s j", j=4))
                        a3 = ACC[:, 0:64].rearrange("p (s j) -> p s j", j=4)
                        self.ts("dve", a3, RAWS[:, :, 0:4], cw(ch, 0), A.mult, cbias(ch), A.add)
                        for jj in range(1, 4):
                            self.stt(a3, RAWS[:, :, jj:jj + 4], cw(ch, jj), a3, A.mult, A.add)
                        self.act(DST[:, j, t0:t0 + tw], ACC[:, 0:64], AF.Silu)
                        if which == 0:
                            self.act(QS32[:, j, :], ACC[:, 0:64], AF.Silu)
                        self.dma("sp", self.o_convs[l, :, ch], RAWS[:, :, 4:7], "oconvs", reads=[RAWS[:, :, 4:7]])
                self.dense_multi([(self.w_in[l], c0, 8, lambda kc, t0, tw: XN[:, kc, t0:t0 + tw])], 256, cons_qk)
            self.dump("COLS_c%d" % hd, COLS)
            self.ws_live = 1
            WV = self.wget(self.w_in[l], C_V + hd * 256, 256, 8)
            for ci, (t0, tw) in enumerate(self.chunks):
                pv = self.bank()[0:tw, 0:256]
                self.mm(pv, [(XN[:, kc, t0:t0 + tw], WV[:, kc, :]) for kc in range(8)])
                self.copy("act" if ci % 2 == 0 else "dve", VT[0:tw, ci, 0:256], pv)
            self.dump("COLS_d%d" % hd, COLS)
            Cn = self.CST[:, l * 4 + hd]
            if isA:
                self.memset("dve", Cn, 0.0)
            self.copy("dve", CnB, Cn)


            def rb():
                b = self.banks[self.bi % 5]
                self.bi += 1
                return b

            S1B = [self.banks[0], self.banks[1], self.banks[2], self.banks[3]]
            S2B = [self.banks[4], self.banks[5], self.banks[6]]

            def rb():
                b = S2B[self.bi % 3]
                self.bi += 1
                return b

            def stage1(ci, t0, P, bm, kw_out, n1_bank, n1_sb=None):
                ST = S1B[0][0:P, 0:P]
                self.mm(ST, [(KT[:, dc, t0:t0 + P], QT[:, dc, t0:t0 + P]) for dc in range(2)]); yield
                EB = S1B[1][0:P, 0:P]
                self.mm(EB, [(ohb(hd, P), MH[:, t0:t0 + P]), (ohb(hd, P), ML_[:, t0:t0 + P])]); yield
                kb = S1B[3][:, 0:128].bitcast(BF16)
                self.transposes([(kb[0:P, dc * 128:(dc + 1) * 128], KT[:, dc, t0:t0 + P], self.identb) for dc in range(2)]); yield
                self.stt(ARG[0:P, 0:P], EB, col(ci, 0, hd, P), bm, A.subtract, A.add); yield
                self.act(Wt[0:P, 0:P], ARG[0:P, 0:P], AF.Exp, scale=-1.0); yield
                self.tt("dve", PT[0:P, 0:P], ST, Wt[0:P, 0:P], A.mult); yield
                N1 = S1B[2][0:P, 0:257]
                self.mm(N1, [(PT[0:P, 0:P], VT[0:P, ci, :])]); yield
                self.act(kw_out[0:P, :], kb[0:P, :], AF.Copy, scale=col(ci, 3, hd, P)); yield
                self.copy("act", n1_sb[0:P, :], N1); yield

            def finish(ci, t0, P, N1, N2src, hbank):
                self.stt(TOT[0:P, :], N2src, col(ci, 1, hd, P), N1, A.mult, A.add); yield
                A_ = ALU
                S257, Tm, Av, Vv, Sc = [smn[0:P, i:i + 1] for i in range(5)]
                SQ = T2
                self.tt("dve", SQ[0:P, :], TOT[0:P, :], TOT[0:P, :], A_.mult); yield
                self.op("dve", lambda e: e.reduce_sum(out=S257, in_=SQ[0:P, :], axis=mybir.AxisListType.X), reads=[SQ[0:P, :]], writes=[S257]); yield
                self.ts("dve", Tm, SQ[0:P, 256:257], col(ci, 2, hd, P), A_.max); yield
                self.tt("dve", Av, S257, SQ[0:P, 256:257], A_.subtract); yield
                self.stt(Vv, Tm, EPS * float(DH), Av, A_.mult, A_.add); yield
                self.act(Vv, Vv, AF.Ln, scale=1.0 / DH); yield
                self.act(Sc, Vv, AF.Exp, scale=-0.5); yield
                self.act(HN[0:P, :], TOT[0:P, 0:256], AF.Copy, scale=Sc); yield
                hb = hbank[:, 0:128].bitcast(BF16)
                self.transposes([(hb[:, dc * 128: dc * 128 + P], HN[0:P, dc * 128:(dc + 1) * 128], self.identb[0:P, 0:P]) for dc in range(2)]); yield
                self.copy("act", HMT[:, hd * 2: hd * 2 + 2, t0:t0 + P], hb[:, :].rearrange("p (d t) -> p d t", d=2)[:, :, 0:P]); yield

            def stage2(ci, t0, N1, kw):
                N2 = S2B[0][:, 0:257]
                self.mm(N2, [(QT[:, kc, t0:t0 + 128], CnB[:, kc, :]) for kc in range(2)]); yield
                for kc in range(2):
                    U = S2B[1 + kc][:, 0:257]
                    self.mm(U, [(kw[:, kc * 128:(kc + 1) * 128], VT[:, ci, :])]); yield
                    self.stt(Cn[:, kc, :], Cn[:, kc, :], DECB[:, hd, ci:ci + 1], U, A.mult, A.add); yield
                self.copy("act", CnB, Cn); yield
                yield from finish(ci, t0, 128, N1, N2, S2B[0])

            def run_il(gens):
                gens = [g for g in gens if g is not None]
                while gens:
                    for g in list(gens):
                        try:
                            next(g)
                        except StopIteration:
                            gens.remove(g)

            def sample_seq(sq):
                ci = nchp
                cs = CS[sq % 2]; co = CO[sq % 2]
                self.dma("sp", cs[:, :, 0:256], self.c0s_d[l, sq, hd].rearrange("(kc p) v -> p kc v", p=128), ("csin", sq % 2), writes=[cs[:, :, 0:256]])
                self.copy("dve", cs[:, :, 256], N0T[:, hd, :, sq])
                vbd = VBD[sq % 2]
                self.act(vbd[0:64, 0:257], VT[0:64, ci, :], AF.Copy, scale=cst[0:64, CC_BLK + sq: CC_BLK + sq + 1])
                self.copy("act", CSB, cs)
                N2s = rb()[0:64, 0:257]
                self.mm(N2s, [(QT[:, kc, NP:NP + 64], CSB[:, kc, :]) for kc in range(2)])
                self.stt(N2ACC[0:64, :], N2s, cst[0:64, CC_BLK + sq: CC_BLK + sq + 1], N2ACC[0:64, :], A.mult, A.add)
                for kc in range(2):
                    U = rb()[:, 0:257]
                    self.mm(U, [(KWS[0:64, kc * 128:(kc + 1) * 128], vbd[0:64, 0:257])])
                    self.stt(co[:, kc, :], cs[:, kc, :], DECB[:, hd, 8 + sq: 9 + sq], U, A.mult, A.add)
                self.dma("sp", self.o_cs[l, sq, hd].rearrange("(kc p) v -> p kc v", p=128), co[:, :, 0:256], ("csout", sq % 2), reads=[co[:, :, 0:256]])
                self.copy("dve", NOUT[:, hd, :, sq], co[:, :, 256])

            if isA:
                run_il([stage1(nchp, NP, 64, bigms, KWS, None, N1S)])
                self.memset("dve", N2ACC[0:64, :], 0.0)
            run_il([stage1(0, 0, 128, bigm, KW2[0], None, N1SB[0])])
            for ci in range(nchp):
                g1 = stage1(ci + 1, (ci + 1) * 128, 128, bigm, KW2[(ci + 1) % 2], None, N1SB[(ci + 1) % 2]) if ci + 1 < nchp else None
                run_il([stage2(ci, ci * 128, N1SB[ci % 2][0:128, :], KW2[ci % 2]), g1])
                if isA:
                    for sq in range(2 * ci, 2 * ci + 2):
                        sample_seq(sq)
            if not isA:
                self.dma("sp", self.o_cp[l, hd].rearrange("(kc p) v -> p kc v", p=128), Cn[:, :, 0:256], "ocp", reads=[Cn[:, :, 0:256]])
                self.copy("dve", NOUTP[:, hd, :], Cn[:, :, 256])
            else:
                run_il([finish(nchp, NP, 64, N1S[0:64, :], N2ACC[0:64, :], S2B[0])])
        accb = ACC.bitcast(BF16)
        sgt = [accb[:, 0:512], accb[:, 512:1024]]
        cnt = [0]

        def cons_o(j, ti, t0, tw, pss):
            t = sgt[cnt[0] % 2]; cnt[0] += 1
            self.act(t[:, 0:tw], pss[0], AF.Sigmoid)
            self.stt(HMT[:, j, t0:t0 + tw], HMT[:, j, t0:t0 + tw], mixc[:, l, MC_GMH + j: MC_GMH + j + 1], t[:, 0:tw], A.mult, A.mult)
        self.dense_multi([(self.w_in[l], C_O, 8, lambda kc, t0, tw: XN[:, kc, t0:t0 + tw])], D, cons_o)
        self.dump("HMT", HMT, BF16); self.dump("CST", self.CST[:, l * 4: l * 4 + 4].rearrange("p a b c -> p (a b c)")); self.dump("QT", QT, BF16); self.dump("KT", KT, BF16); self.dump("VT", VT, BF16)
        if isA:
            self.dma("sp", self.o_ns[l], NOUT, "ons", reads=[NOUT])
            self.dma("sp", self.o_ms[l], MNEW[:, 8:24], "oms", reads=[MNEW[:, 8:24]])
        else:
            self.dma("sp", self.o_np[l], NOUTP, "onp", reads=[NOUTP])
            self.dma("sp", self.o_mp[l], MNEW[:, nchp - 1:nchp], "omp", reads=[MNEW[:, nchp - 1:nchp]])
            self.dma("sp", self.o_convp[l], self.convt[:, l], "oconvp", reads=[self.convt[:, l]])
        self.release(m0)

    def _final(self):
        m = self.mark()
        XF = self.alloc([8, 128], F32)
        sq = self.alloc([8, 128], BF16)
        rs = self.alloc([128], F32)
        yo = [self.alloc([D], F32) for _ in range(2)]
        gi = 4 * NL
        for ci, (t0, tw) in enumerate(self.chunks):
            self.act(sq[:, :, 0:tw], self.xT[:, :, t0:t0 + tw], AF.Square)
            ps = self.bank()[:, 0:tw]
            self.mm(ps, [(self.onesb[:], sq[:, kc, 0:tw]) for kc in range(8)])
            self.act(rs[:, 0:tw], ps, AF.Sqrt, bias=EPS, scale=1.0 / D)
            self.recip(rs[:, 0:tw], rs[:, 0:tw])
            for kc in range(8):
                self.stt(XF[:, kc, 0:tw], self.xT[:, kc, t0:t0 + tw], self.gcols[:, gi, kc:kc + 1], rs[:, 0:tw], ALU.mult, ALU.mult)
            y = yo[ci % 2]
            for half in range(2):
                b = self.bank()
                for q in range(4):
                    kc = half * 4 + q
                    self.transpose(b[0:tw, q * 128:(q + 1) * 128], XF[:, kc, 0:tw], self.ident)
                self.copy("act" if half == 0 else "dve", y[0:tw, half * 512:(half + 1) * 512], b[0:tw, :])
            dst = self.y_p[self.p0 + t0: self.p0 + t0 + tw, :] if t0 < TPP else self.y_s[:, :]
            self.dma("sp", dst, y[0:tw, :], ("yout", ci % 2), reads=[y[0:tw, :]])
        self.release(m)


def nc_rem(nc):
    r = nc.sbuf_bytes_remaining
    return r() if callable(r) else r


def _host_consts():
    c = np.zeros((128, CC_N), np.float32)
    c[:, CC_ID:CC_ID + 128] = np.eye(128, dtype=np.float32)
    c[:, CC_J:CC_J + 128] = np.arange(128, dtype=np.float32)[None, :]
    for h in range(4):
        c[h, CC_OH + 128 * h: CC_OH + 128 * (h + 1)] = 1.0
    sidx = np.arange(128)[:, None]; tidx = np.arange(128)[None, :]
    c[:, CC_BM:CC_BM + 128] = np.where(sidx <= tidx, LN16, 80.0)
    s64 = np.arange(64)[:, None]; t64 = np.arange(64)[None, :]
    c[0:64, CC_BMS:CC_BMS + 64] = np.where((s64 // 4 == t64 // 4) & (s64 <= t64), LN16, 80.0)
    c[0:64, CC_BLK:CC_BLK + 16] = (s64 // 4 == np.arange(16)[None, :]).astype(np.float32)
    m01 = np.ones(TPP + NSAMP, np.float32)
    m01[0:TPP:128] = 0.0; m01[TPP::4] = 0.0
    c[:, CC_M01:CC_M01 + TPP + NSAMP] = m01[None, :]
    m4 = np.ones(64, np.float32); m4[0::4] = 0.0
    c[:, CC_M4:CC_M4 + 64] = m4[None, :]
    for qi in range(4):
        for h in range(4):
            c[h, CC_SEL + 16 * qi + 4 * qi + h] = 1.0
    return c


def _gn(a):
    a = np.asarray(a, np.float32)
    rest = a.shape[2:]
    a = a.reshape((16, 2, 64) + rest)
    a = np.moveaxis(a, 0, 2)
    return np.ascontiguousarray(a.reshape((128, 16) + rest))


def _s5_layouts(inp, l):
    sp = np.zeros((128, 3, 16), np.float32)
    sp[:, 0] = _gn(inp["s5_lambda_re"][l]); sp[:, 1] = _gn(inp["s5_lambda_im"][l])
    sp[:, 2] = _gn(np.repeat(np.asarray(inp["s5_log_dt"][l])[:, None], 64, axis=1))
    Bbd = np.zeros((2, 128, 16, 128), np.float32); Cbd = np.zeros((2, 128, 16, 128), np.float32)
    for part, (bn, cn) in enumerate([("s5_b_re", "s5_c_re"), ("s5_b_im", "s5_c_im")]):
        Bg = _gn(inp[bn][l])
        Cg = _gn(np.transpose(np.asarray(inp[cn][l]), (0, 2, 1)))
        for gc in range(16):
            for gl in range(2):
                c0 = 32 * (gc % 4) + 16 * gl
                Bbd[part, 64 * gl:64 * gl + 64, gc, c0:c0 + 16] = Bg[64 * gl:64 * gl + 64, gc, :]
                Cbd[part, 64 * gl:64 * gl + 64, gc, c0:c0 + 16] = Cg[64 * gl:64 * gl + 64, gc, :]
    return sp, Bbd, Cbd


def _core_inputs(inp, core, cfg=None):
    f = lambda a: np.ascontiguousarray(a, dtype=np.float32)
    m = {}
    m["x_p"] = f(inp["x_prompt"][core])
    m["x_s"] = f(inp["x_sample"][16 * core:16 * core + 16].reshape(NSAMP, D))
    m["p_p"] = f(inp["p_prompt"][:, core])
    m["p_s"] = f(inp["p_sample"][:, 16 * core:16 * core + 16].reshape(NL, NSAMP, DPLE))
    for n in ["w1_gate", "w1_up", "w1_down", "w_in", "s5_w_glu", "w_s5_up", "w_m_up", "w_out", "w2_gate", "w2_up",
              "w2_down", "w_ple", "w_ple_gate"]:
        m[n] = f(inp[n])
    g = np.zeros((128, 4 * NL + 1, 8), np.float32)
    for l in range(NL):
        for k, n in enumerate(["g_ffn1", "g_mix", "g_ffn2", "g_ple"]):
            g[:, 4 * l + k, :] = np.asarray(inp[n][l]).reshape(8, 128).T
    g[:, 4 * NL, :] = np.asarray(inp["g_final"]).reshape(8, 128).T
    m["gcols"] = g
    m["consts"] = _host_consts()
    s5p = np.zeros((NL, 128, 3, 16), np.float32); s5B = np.zeros((NL, 2, 128, 16, 128), np.float32); s5C = np.zeros_like(s5B)
    mixc = np.zeros((128, NL, MIXC_N), np.float32)
    s5h0 = np.zeros((NL, 2, 128, 16, 16), np.float32)
    sl = slice(16 * core, 16 * core + 16)
    for l in range(NL):
        s5p[l], s5B[l], s5C[l] = _s5_layouts(inp, l)
        mixc[:, l, MC_D:MC_D + 4] = np.asarray(inp["s5_d"][l]).reshape(4, 128).T
        mixc[:, l, MC_CW:MC_CW + 64] = np.asarray(inp["conv_w"][l]).reshape(4, 16, 128).transpose(2, 1, 0).reshape(128, 64)
        mixc[:, l, MC_CB:MC_CB + 16] = np.asarray(inp["conv_b"][l]).reshape(16, 128).T
        mixc[:, l, MC_GMH:MC_GMH + 8] = np.asarray(inp["g_mhead"][l]).reshape(8, 128).T
        mixc[0:4, l, MC_BI] = np.asarray(inp["b_igate"][l]); mixc[0:4, l, MC_BF] = np.asarray(inp["b_fgate"][l])
        for part, n in enumerate(["state_s5_re", "state_s5_im"]):
            st = np.asarray(inp[n][l, sl])
            s5h0[l, part] = _gn(np.moveaxis(st, 0, 2))
    m["s5p"] = s5p; m["s5B"] = s5B; m["s5C"] = s5C; m["mixc"] = mixc; m["s5h0"] = s5h0
    m["c0s"] = f(inp["state_mlstm_c"][:, sl])
    n0 = np.asarray(inp["state_mlstm_n"][:, sl], np.float32)
    m["n0T"] = np.ascontiguousarray(n0.reshape(NL, 16, 4, 2, 128).transpose(4, 0, 2, 3, 1))
    m0 = np.zeros((128, NL, 16), np.float32)
    m0[0:4] = np.asarray(inp["state_mlstm_m"][:, sl], np.float32).transpose(2, 0, 1)
    m["m0T"] = m0
    cv = np.asarray(inp["state_conv"][:, sl], np.float32)
    m["conv0T"] = np.ascontiguousarray(cv.reshape(NL, 16, 3, 16, 128).transpose(0, 4, 3, 1, 2))
    return m


def run(inputs, cfg=None, cores=None, trace=False):
    cores = list(range(NCORES)) if cores is None else cores
    prog = Prog(cfg)
    nc = prog.build()
    in_maps = []
    for c in cores:
        cm = _core_inputs(inputs, c, cfg)
        in_maps.append({k: cm[k] for k in prog.ins})
    res = run_bass_kernel_spmd(nc, in_maps, core_ids=list(range(len(cores))), trace=trace)
    return prog, res


def _ungn(a):
    rest = a.shape[2:]
    a = a.reshape((2, 64, 16) + rest)
    a = np.moveaxis(a, 2, 0)
    return a.reshape((32, 64) + rest)


def kernel(**inputs):
    prog, res = run(inputs)
    L = NL
    B = NCORES; SB = 16 * NCORES
    y_p = np.zeros((B, SEQ, D), np.float32); y_s = np.zeros((SB, 4, D), np.float32)
    s5p = [np.zeros((L, B, 32, 64), np.float32) for _ in range(2)]
    s5s = [np.zeros((L, SB, 32, 64), np.float32) for _ in range(2)]
    c_p = np.zeros((L, B, 4, 256, 256), np.float32); n_p = np.zeros((L, B, 4, 256), np.float32)
    m_p = np.zeros((L, B, 4), np.float32); cv_p = np.zeros((L, B, 3, 2048), np.float32)
    c_s = np.zeros((L, SB, 4, 256, 256), np.float32); n_s = np.zeros((L, SB, 4, 256), np.float32)
    m_s = np.zeros((L, SB, 4), np.float32); cv_s = np.zeros((L, SB, 3, 2048), np.float32)
    for c in range(NCORES):
        r = res.results[c]
        sl = slice(16 * c, 16 * c + 16)
        y_p[c] = r["y_p"]; y_s[sl] = np.asarray(r["y_s"]).reshape(16, 4, D)
        for l in range(L):
            for part in range(2):
                s5p[part][l, c] = _ungn(np.asarray(r["o_s5p"][l][:, part]))
                s5s[part][l, sl] = np.moveaxis(_ungn(np.asarray(r["o_s5s"][l][:, part])), 2, 0)
            c_p[l, c] = r["o_cp"][l]
            n_p[l, c] = np.asarray(r["o_np"][l]).transpose(1, 2, 0).reshape(4, 256)
            m_p[l, c] = np.asarray(r["o_mp"][l])[:, 0]
            cv_p[l, c] = np.asarray(r["o_convp"][l]).transpose(2, 1, 0).reshape(3, 2048)
            c_s[l, sl] = r["o_cs"][l]
            n_s[l, sl] = np.asarray(r["o_ns"][l]).transpose(3, 1, 2, 0).reshape(16, 4, 256)
            m_s[l, sl] = np.asarray(r["o_ms"][l]).T
            cv_s[l, sl] = np.asarray(r["o_convs"][l]).transpose(2, 3, 1, 0).reshape(16, 3, 2048)
    return (y_p, y_s, s5p[0], s5p[1], c_p, n_p, m_p, cv_p, s5s[0], s5s[1], c_s, n_s, m_s, cv_s)
```
